# Optimizing a Trainium2 kernel written in Bass

```python
import jax, jax.numpy as jnp
from jax import lax
import numpy as np

D_MODEL = 2048
BATCH = 4
SEQ = 2048
DEPTH = 2

MIX_WIDTH = D_MODEL
GLA_HEADS = 4
GLA_DV = MIX_WIDTH // 2 // GLA_HEADS
GLA_DK = GLA_DV // 2
GLA_GATE_RANK = 16
GLA_GATE_TAU = 16.0
GLA_CHUNK = 64
DIL_HEADS = 8
DIL_DH = (MIX_WIDTH - GLA_HEADS * GLA_DV) // DIL_HEADS
DIL_PATTERNS = ((128, 1), (512, 4), (2048, 16))
DIL_BLOCK = 128
REL_BUCKETS = 32
REL_MAX_DIST = 2048
FFN_HIDDEN = -(-8 * D_MODEL // (3 * 256)) * 256
RMS_EPS = 1e-6
NEG_INF = -1e30

SPLIT_SIZES = (
    GLA_HEADS * GLA_DK,
    GLA_HEADS * GLA_DK,
    GLA_HEADS * GLA_DV,
    GLA_HEADS * GLA_DV,
    GLA_GATE_RANK,
    DIL_HEADS * DIL_DH,
    DIL_HEADS * DIL_DH,
    DIL_HEADS * DIL_DH,
)
N_IN = sum(SPLIT_SIZES)

kernel_name = "hybrid_gla_dilated_parallel_heads"


def rms_norm(x, g):
    xf = x.astype(jnp.float32)
    y = xf * lax.rsqrt(jnp.mean(xf * xf, axis=-1, keepdims=True) + RMS_EPS)
    return (y * g.astype(jnp.float32)).astype(x.dtype)


def t5_bucket(dist):
    max_exact = REL_BUCKETS // 2
    safe = np.maximum(dist, 1)
    large = max_exact + (np.log(safe / max_exact) / np.log(REL_MAX_DIST / max_exact)
                         * (REL_BUCKETS - max_exact)).astype(np.int64)
    large = np.minimum(large, REL_BUCKETS - 1)
    return np.where(dist < max_exact, dist, large).astype(np.int32)


def gla_mixer(q, k, v, r, log_g, onorm_g):
    Bsz, S, H, DK = q.shape
    DV = v.shape[-1]
    C = GLA_CHUNK
    N = S // C

    def chunks(t):
        return t.astype(jnp.float32).reshape(Bsz, N, C, H, t.shape[-1]).transpose(0, 3, 1, 2, 4)

    qc = chunks(q) * (DK ** -0.5)
    kc, vc, gc = chunks(k), chunks(v), chunks(log_g)
    b = jnp.cumsum(gc, axis=3)
    q_t = qc * jnp.exp(b)
    k_t = kc * jnp.exp(-b)
    causal = np.tril(np.ones((C, C), dtype=bool))
    attn = jnp.einsum('bhnid,bhnjd->bhnij', q_t, k_t)
    attn = jnp.where(causal, attn, 0.0)
    o_intra = jnp.einsum('bhnij,bhnjv->bhniv', attn, vc)

    b_last = b[:, :, :, -1, :]
    k_dec = kc * jnp.exp(b_last[:, :, :, None, :] - b)
    chunk_state = jnp.einsum('bhncd,bhncv->bhndv', k_dec, vc)

    def step(state, inp):
        q_n, decay_n, cs_n = inp
        o_n = jnp.einsum('bhcd,bhdv->bhcv', q_n, state)
        state = decay_n[..., None] * state + cs_n
        return state, o_n

    init = jnp.zeros((Bsz, H, DK, DV), jnp.float32)
    xs = (q_t.transpose(2, 0, 1, 3, 4), jnp.exp(b_last).transpose(2, 0, 1, 3),
          chunk_state.transpose(2, 0, 1, 3, 4))
    _, o_inter = lax.scan(step, init, xs)
    o = o_intra + o_inter.transpose(1, 2, 0, 3, 4)
    o = o.transpose(0, 2, 3, 1, 4).reshape(Bsz, S, H, DV)
    o = rms_norm(o, onorm_g)
    o = o.reshape(Bsz, S, H * DV) * jax.nn.silu(r.astype(jnp.float32))
    return o


def dilated_branch(q, k, v, rel_bias, window, dilation):
    Bsz, H, S, Dh = q.shape
    span = window // dilation
    Q = DIL_BLOCK
    seg = dilation * Q
    Sp = -(-S // seg) * seg
    L = Sp // dilation
    nb = L // Q

    def to_blocks(t):
        t = jnp.pad(t, ((0, 0), (0, 0), (0, Sp - S), (0, 0)))
        return t.reshape(Bsz, H, L, dilation, Dh).transpose(0, 1, 3, 2, 4).reshape(Bsz, H, dilation, nb, Q, Dh)

    qb, kb, vb = to_blocks(q), to_blocks(k), to_blocks(v)
    blk_pad = ((0, 0), (0, 0), (0, 0), (1, 0), (0, 0), (0, 0))
    kk = jnp.concatenate([jnp.pad(kb[:, :, :, :-1], blk_pad), kb], axis=4)
    vv = jnp.concatenate([jnp.pad(vb[:, :, :, :-1], blk_pad), vb], axis=4)

    i = np.arange(Q)[:, None]
    j = np.arange(2 * Q)[None, :]
    rel = Q + i - j
    band = (rel >= 0) & (rel <= span)
    valid = band[None] & ((np.arange(nb)[:, None, None] > 0) | (j >= Q)[None])
    buckets = t5_bucket(np.clip(rel, 0, None) * dilation)
    bias = jnp.take(rel_bias.astype(jnp.float32), buckets, axis=0).transpose(2, 0, 1)

    logits = jnp.einsum('bhrnqd,bhrnkd->bhrnqk', qb, kk) * (Dh ** -0.5)
    logits = logits + bias[None, :, None, None]
    logits = jnp.where(valid[None, None, None], logits, NEG_INF)
    m = jnp.max(logits, axis=-1, keepdims=True)
    p = jnp.exp(logits - m)
    s = jnp.sum(p, axis=-1, keepdims=True)
    o = jnp.einsum('bhrnqk,bhrnkd->bhrnqd', p, vv) / s
    lse = (m + jnp.log(s))[..., 0]

    o = o.reshape(Bsz, H, dilation, L, Dh).transpose(0, 1, 3, 2, 4).reshape(Bsz, H, Sp, Dh)[:, :, :S]
    lse = lse.reshape(Bsz, H, dilation, L).transpose(0, 1, 3, 2).reshape(Bsz, H, Sp)[:, :, :S]
    return o, lse


def dilated_mixer(q, k, v, rel_bias):
    Bsz, S, H, Dh = q.shape
    qf, kf, vf = (t.astype(jnp.float32).transpose(0, 2, 1, 3) for t in (q, k, v))
    outs, lses = [], []
    for window, dilation in DIL_PATTERNS:
        o, lse = dilated_branch(qf, kf, vf, rel_bias, window, dilation)
        outs.append(o)
        lses.append(lse)
    w = jax.nn.softmax(jnp.stack(lses, axis=0), axis=0)
    o = jnp.sum(w[..., None] * jnp.stack(outs, axis=0), axis=0)
    return o.transpose(0, 2, 1, 3).reshape(Bsz, S, H * Dh)


def setup_inputs(seed: int = 0) -> dict:
    key = jax.random.key(seed)
    ks = jax.random.split(key, 16)
    f32 = jnp.float32
    nrm = lambda k, shape, scale: jax.random.normal(k, shape, f32) * scale
    return {
        "x": nrm(ks[0], (BATCH, SEQ, D_MODEL), 1.0),
        "norm1_g": 1.0 + nrm(ks[1], (DEPTH, D_MODEL), 0.02),
        "w_in": nrm(ks[2], (DEPTH, D_MODEL, N_IN), D_MODEL ** -0.5),
        "gla_gate_w2": nrm(ks[3], (DEPTH, GLA_GATE_RANK, GLA_HEADS * GLA_DK), GLA_GATE_RANK ** -0.5),
        "gla_gate_b": nrm(ks[4], (DEPTH, GLA_HEADS * GLA_DK), 0.1),
        "gla_onorm_g": 1.0 + nrm(ks[5], (DEPTH, GLA_DV), 0.02),
        "q_norm_g": 1.0 + nrm(ks[6], (DEPTH, DIL_DH), 0.02),
        "k_norm_g": 1.0 + nrm(ks[7], (DEPTH, DIL_DH), 0.02),
        "rel_bias": nrm(ks[8], (REL_BUCKETS, DIL_HEADS), 0.5),
        "w_out": nrm(ks[9], (DEPTH, MIX_WIDTH, D_MODEL), MIX_WIDTH ** -0.5),
        "norm2_g": 1.0 + nrm(ks[10], (DEPTH, D_MODEL), 0.02),
        "w_gate": nrm(ks[11], (DEPTH, D_MODEL, FFN_HIDDEN), D_MODEL ** -0.5),
        "w_up": nrm(ks[12], (DEPTH, D_MODEL, FFN_HIDDEN), D_MODEL ** -0.5),
        "w_down": nrm(ks[13], (DEPTH, FFN_HIDDEN, D_MODEL), FFN_HIDDEN ** -0.5),
    }


def reference(x, norm1_g, w_in, gla_gate_w2, gla_gate_b, gla_onorm_g, q_norm_g, k_norm_g,
              rel_bias, w_out, norm2_g, w_gate, w_up, w_down):
    Bsz, S, _ = x.shape
    split_points = list(np.cumsum(SPLIT_SIZES)[:-1])
    for l in range(DEPTH):
        h = rms_norm(x, norm1_g[l])
        proj = h @ w_in[l]
        qa, ka, va, ra, ga, qb, kb, vb = jnp.split(proj, split_points, axis=-1)
        gate_pre = (ga @ gla_gate_w2[l] + gla_gate_b[l]).astype(jnp.float32)
        log_g = jax.nn.log_sigmoid(gate_pre) / GLA_GATE_TAU
        out_a = gla_mixer(qa.reshape(Bsz, S, GLA_HEADS, GLA_DK),
                          ka.reshape(Bsz, S, GLA_HEADS, GLA_DK),
                          va.reshape(Bsz, S, GLA_HEADS, GLA_DV),
                          ra,
                          log_g.reshape(Bsz, S, GLA_HEADS, GLA_DK),
                          gla_onorm_g[l])
        qb = rms_norm(qb.reshape(Bsz, S, DIL_HEADS, DIL_DH), q_norm_g[l])
        kb = rms_norm(kb.reshape(Bsz, S, DIL_HEADS, DIL_DH), k_norm_g[l])
        out_b = dilated_mixer(qb, kb, vb.reshape(Bsz, S, DIL_HEADS, DIL_DH), rel_bias)
        mix = jnp.concatenate([out_a.astype(x.dtype), out_b.astype(x.dtype)], axis=-1)
        x = x + mix @ w_out[l]
        h2 = rms_norm(x, norm2_g[l])
        x = x + (jax.nn.silu(h2 @ w_gate[l]) * (h2 @ w_up[l])) @ w_down[l]
    return x
```

```python
import numpy as np
import concourse.bass as bass
import concourse.mybir as mybir
from concourse.bass_utils import run_bass_kernel_spmd

F32 = mybir.dt.float32
BF16 = mybir.dt.bfloat16
AF = mybir.ActivationFunctionType
ALU = mybir.AluOpType

D = 2048
T = 1024
NCH = 16
N_IN = 6160
FFN = 5632
NJ = 44
EPS = 1e-6
NEG = -200.0
C_QA, C_KA, C_VA, C_RA, C_GA, C_QB, C_KB, C_VB = 0, 512, 1024, 2048, 3072, 3088, 4112, 5136
NSLOT = 5
NS_DMA = 8
FFN_PARTS = ((0, 24), (24, 20))


class _Op:
    __slots__ = ("eng", "fn", "deps", "signal", "sig", "kind", "qidx")


class Sched:
    ENGS = ("pe", "act", "dve", "pool", "sp")

    def __init__(self, nc):
        self.nc = nc
        self.ops = []
        self.lastw = {}
        self.rd_c = {}
        self.rd_d = {}
        self.pending = {}
        self.last_op = {e: None for e in self.ENGS}
        self.dmas = []
        self.qcount = {"pool": 0, "sp": 0}
        self.ncc = 0

    def add(self, eng, fn, reads=(), writes=(), kind="c", nobarrier=False):
        op = _Op()
        op.eng, op.fn, op.kind, op.deps, op.signal, op.sig, op.qidx = eng, fn, kind, set(), False, None, None
        for r in reads:
            w = self.lastw.get(r)
            if w is not None:
                op.deps.add(w)
        for k in writes:
            w = self.lastw.get(k)
            if w is not None:
                op.deps.add(w)
            for o in self.rd_c.get(k, {}).values():
                op.deps.add(o)
            for o in self.rd_d.get(k, ()):
                op.deps.add(o)
        if not nobarrier and eng in self.pending:
            op.deps |= self.pending.pop(eng)
        for r in reads:
            if kind == "c":
                self.rd_c.setdefault(r, {})[eng] = op
            else:
                self.rd_d.setdefault(r, []).append(op)
        for k in writes:
            self.lastw[k] = op
            self.rd_c[k] = {}
            self.rd_d[k] = []
        if kind == "dma":
            op.qidx = self.qcount[eng]
            self.qcount[eng] += 1
        elif kind == "cc":
            op.qidx = self.ncc
            self.ncc += 1
        if kind != "c":
            self.dmas.append(op)
        self.ops.append(op)
        self.last_op[eng] = op
        return op

    def barrier(self):
        deps = set(o for o in self.last_op.values() if o is not None) | set(self.dmas)
        for e in self.ENGS:
            self.pending[e] = set(deps) | self.pending.get(e, set())
        self.dmas = []

    def emit(self):
        nc = self.nc
        for op in self.ops:
            for d in op.deps:
                if d.kind == "c" and d.eng == "pe" and op.eng == "pe" and op.kind == "c":
                    continue
                d.signal = True
        import contextlib
        with contextlib.ExitStack() as st:
            csem = {e: st.enter_context(nc.semaphore("c_" + e)) for e in ("pe", "act", "dve")}
            qsem = {q: [st.enter_context(nc.semaphore("q_%s%d" % (q, i))) for i in range(NS_DMA)]
                    for q in ("pool", "sp")}
            ccsem = [st.enter_context(nc.semaphore("cc%d" % i)) for i in range(max(1, self.ncc))]
            cnt = {e: 0 for e in csem}
            byq = {"pool": [], "sp": []}
            for op in self.ops:
                if op.kind == "c":
                    if op.signal:
                        cnt[op.eng] += 1
                        op.sig = (csem[op.eng], cnt[op.eng], 1)
                elif op.kind == "dma":
                    i = op.qidx
                    op.sig = (qsem[op.eng][i % NS_DMA], 16 * (i // NS_DMA + 1), 16)
                    byq[op.eng].append(op)
                else:
                    op.sig = (ccsem[op.qidx], 1, None)
            block = st.enter_context(nc.Block())
            ops = self.ops

            def run(engname):
                def body(e):
                    known = {}
                    for op in ops:
                        if op.eng != engname:
                            continue
                        need = {}
                        for d in op.deps:
                            if d.sig is None:
                                continue
                            if d.kind == "c" and d.eng == "pe" and engname == "pe" and op.kind == "c":
                                continue
                            s, v, _ = d.sig
                            if need.get(id(s), (None, 0))[1] < v:
                                need[id(s)] = (s, v)
                        if op.kind == "dma" and op.qidx >= NS_DMA:
                            p = byq[engname][op.qidx - NS_DMA]
                            s, v, _ = p.sig
                            if need.get(id(s), (None, 0))[1] < v:
                                need[id(s)] = (s, v)
                        for sid, (s, v) in need.items():
                            if known.get(sid, 0) >= v:
                                continue
                            e.wait_ge(s, v)
                            known[sid] = v
                        ins = op.fn(e)
                        if op.kind == "c":
                            if op.signal:
                                ins.then_inc(op.sig[0], 1)
                        elif op.kind == "dma":
                            ins.then_inc(op.sig[0], 16)
                        else:
                            ins.then_inc(op.sig[0])
                    if engname in byq:
                        last = {}
                        for op in byq[engname]:
                            last[id(op.sig[0])] = (op.sig[0], op.sig[1])
                        for sid, (s, v) in last.items():
                            if known.get(sid, 0) < v:
                                e.wait_ge(s, v)
                return body

            block.tensor(run("pe"))
            block.scalar(run("act"))
            block.vector(run("dve"))
            block.gpsimd(run("pool"))
            block.sync(run("sp"))


def _piece(colf, bk, pn):
    ap = colf(bk)
    if ap.shape[-1] != pn:
        ap = ap[:, 0:pn]
    return ap


class Rot:
    def __init__(self, items):
        self.items = list(items)
        self.i = 0

    def next(self):
        v = self.items[self.i % len(self.items)]
        self.i += 1
        return v


def _t5_bucket(dist):
    max_exact = 16
    safe = np.maximum(dist, 1)
    large = max_exact + (np.log(safe / max_exact) / np.log(2048 / max_exact) * (32 - max_exact)).astype(np.int64)
    large = np.minimum(large, 31)
    return np.where(dist < max_exact, dist, large).astype(np.int64)


def _bias_index_and_mask(half):
    j = np.arange(128)[:, None]
    i = np.arange(128)[None, :]
    idx = np.zeros((128, 832), np.int64)
    msk = np.zeros((128, 832), np.float32)
    ph = 0.0 if half == 1 else NEG
    cur_rel = i - j
    prev_rel = 128 + i - j
    cur_ok = (j <= i)
    prev_ok = (i <= j)
    for off, dil in ((0, 1), (256, 4)):
        idx[:, off:off + 128] = _t5_bucket(np.clip(cur_rel, 0, None) * dil)
        msk[:, off:off + 128] = np.where(cur_ok, 0.0, NEG)
        idx[:, off + 128:off + 256] = _t5_bucket(np.clip(prev_rel, 0, None) * dil)
        msk[:, off + 128:off + 256] = np.where(prev_ok, 0.0, NEG)
    i3 = np.arange(64)[None, :]
    rel3 = 64 + i3 - j
    idx[:, 512:576] = _t5_bucket(np.clip(rel3, 0, None) * 16)
    m3 = np.where(rel3 >= 0, 0.0, NEG).astype(np.float32)
    m3[:64, :] += ph
    msk[:, 512:576] = m3
    for off, dil in ((576, 1), (704, 4)):
        idx[:, off:off + 128] = _t5_bucket(np.clip(prev_rel, 0, None) * dil)
        msk[:, off:off + 128] = np.where(prev_ok, 0.0, NEG) + ph
    return idx, msk


def _consts():
    c = np.zeros((128, 512), np.float32)
    j = np.arange(128)[:, None]
    i = np.arange(128)[None, :]
    c[:, 0:128] = np.eye(128, dtype=np.float32)
    c[:, 128:256] = np.where(j <= i, -1.0 / 16.0, 0.0)
    c[:, 256:384] = np.where(j <= i, 1.0, 0.0)
    c[:, 384:512] = 1.0
    return c


class _Stop(Exception):
    pass


def build(layers=(0, 1), dbg=(), stop=None):
    nc = bass.Bass("TRN2", target_bir_lowering=False)
    dt = lambda name, shape, dtype=F32: nc.dram_tensor(name, shape, dtype, kind="ExternalInput").ap()
    xT_d = dt("xT", [D, T])
    NL = len(layers)
    upto = 99 if stop is None else stop
    w_in_d = dt("w_in", [NL, D, N_IN])
    w_out_d = dt("w_out", [NL, D, D]) if upto > 4 else None
    w_gate_d = dt("w_gate", [NL, D, FFN]) if upto > 5 else None
    w_up_d = dt("w_up", [NL, D, FFN]) if upto > 5 else None
    w_down_d = dt("w_down", [NL, FFN, D]) if upto > 5 else None
    gpar_d = dt("gpar", [128, 80])
    w2_d = dt("w2", [16, 2 * 512])
    gb_d = dt("gb", [1, 2 * 512])
    cst_d = dt("cst", [128, 512])
    bias_d = dt("biasT", [8, 128, 832])
    mask_d = dt("maskT", [128, 832])
    yT_d = nc.dram_tensor("yT", [D, T], F32, kind="ExternalOutput").ap()
    dbg_d = {name: nc.dram_tensor("dbg_" + name, list(shape), (BF16 if dtn == "bf16" else F32), kind="ExternalOutput").ap()
             for name, shape, dtn in dbg}

    kloc = [nc.dram_tensor("kloc%d" % l, [1024, 1024], BF16) for l in range(2)]
    vloc = [nc.dram_tensor("vloc%d" % l, [1024, 1024], BF16) for l in range(2)]
    kall = [nc.dram_tensor("kall%d" % l, [2048, 1024], BF16) for l in range(2)]
    vall = [nc.dram_tensor("vall%d" % l, [2048, 1024], BF16) for l in range(2)]
    sfl = [nc.dram_tensor("sfl%d" % l, [512, 256], F32) for l in range(2)]
    sfa = [nc.dram_tensor("sfa%d" % l, [1024, 256], F32) for l in range(2)]
    park_o = [nc.dram_tensor("parko%d" % l, [4, 128, 2048], F32) for l in range(2)]
    park_q = [nc.dram_tensor("parkq%d" % l, [4, 128, 1024], BF16) for l in range(2)]

    import contextlib
    with contextlib.ExitStack() as st:
        sb = lambda name, shape, dtype: st.enter_context(nc.sbuf_tensor(name, shape, dtype))
        xT = sb("xT_sb", [128, NCH, T], F32)
        hT = sb("hT_sb", [128, NCH, T], BF16)
        BIG = sb("big", [128, 16384], F32)
        wsl = [sb("wsl%d" % i, [128, 4096], BF16) for i in range(NSLOT)]
        cst = sb("cst_sb", [128, 512], F32)
        gpar = sb("gpar_sb", [128, 80], F32)
        gsc = sb("gsc_sb", [128, 80], F32)
        cbf = sb("cbf_sb", [128, 256], BF16)
        wga = sb("wga_sb", [128, 2 * NCH * 16], BF16)
        msk = sb("msk_sb", [128, 832], F32)
        ps = [st.enter_context(nc.psum_tensor("ps%d" % i, [128, 1024], F32)) for i in range(4)]

        S = Sched(nc)
        ident_f = cst[:, 0:128]
        tri_f = cst[:, 128:256]
        caus_f = cst[:, 256:384]
        ones_f = cst[:, 384:512]
        ident_b = cbf[:, 0:128]
        ones_b = cbf[:, 128:256]

        def bank(b):
            return ps[b // 2][:, (b % 2) * 512:(b % 2 + 1) * 512]

        def pk(*banks):
            return [("ps", b) for b in banks]

        def bf(off, n):
            return BIG[:, off:off + n // 2].bitcast(BF16)

        def ff(off, n):
            return BIG[:, off:off + n]

        S.add("sp", lambda e: e.dma_start(out=cst[:], in_=cst_d), writes=["cst"], kind="dma")
        S.add("sp", lambda e: e.dma_start(out=gpar[:], in_=gpar_d), writes=["gpar"], kind="dma")
        S.add("sp", lambda e: e.dma_start(out=msk[:], in_=mask_d), writes=["msk"], kind="dma")
        for q in range(4):
            S.add("sp", lambda e, q=q: e.dma_start(
                out=xT[:, 4 * q:4 * q + 4, :],
                in_=xT_d[512 * q:512 * (q + 1), :].rearrange("(c p) t -> p c t", p=128)),
                writes=[("x", c, t) for c in range(4 * q, 4 * q + 4) for t in range(2)], kind="dma")
        for li_, l in enumerate(layers):
            S.add("pool", lambda e, l=l, li_=li_: e.dma_start(
                out=wga[:, l * 256:(l + 1) * 256].rearrange("p (c n) -> p c n", n=16),
                in_=w_in_d[li_, :, C_GA:C_GA + 16].rearrange("(c p) n -> p c n", p=128)),
                writes=[("wga", l)], kind="dma", nobarrier=True)
        S.add("dve", lambda e: e.tensor_copy(out=cbf[:, 0:128], in_=cst[:, 0:128]), reads=["cst"], writes=["cbf0"])
        S.add("dve", lambda e: e.tensor_copy(out=cbf[:, 128:256], in_=cst[:, 384:512]), reads=["cst"], writes=["cbf1"])
        S.add("dve", lambda e: e.tensor_scalar(out=gsc[:, 0:64], in0=gpar[:, 0:64], scalar1=float(np.sqrt(D)),
                                               scalar2=None, op0=ALU.mult), reads=["gpar"], writes=["gsc0"])
        S.add("dve", lambda e: e.tensor_scalar(out=gsc[:, 64:68], in0=gpar[:, 64:68], scalar1=16.0,
                                               scalar2=None, op0=ALU.mult), reads=["gpar"], writes=["gsc1"])
        S.add("dve", lambda e: e.tensor_scalar(out=gsc[:, 68:70], in0=gpar[:, 68:70], scalar1=1.0,
                                               scalar2=None, op0=ALU.mult), reads=["gpar"], writes=["gsc2"])
        S.add("dve", lambda e: e.tensor_scalar(out=gsc[:, 70:72], in0=gpar[:, 70:72], scalar1=float(np.sqrt(128.0)),
                                               scalar2=None, op0=ALU.mult), reads=["gpar"], writes=["gsc3"])
        GS = ["gsc0", "gsc1", "gsc2", "gsc3", "gpar"]
        CB = ["cbf0", "cbf1", "cst"]
        flag = gpar[:, 72:73]

        wseq = []

        def wreq(src, a, b):
            wseq.append((src, a, b))
            return len(wseq) - 1

        wstate = {"issued": 0}

        def wget(i):
            while wstate["issued"] < min(len(wseq), i + NSLOT - 1):
                k = wstate["issued"]
                src, a, b = wseq[k]
                slot = k % NSLOT
                dst = wsl[slot][:, 0:a * b].rearrange("p (a b) -> p a b", b=b)
                S.add("pool", lambda e, dst=dst, src=src: e.dma_start(out=dst, in_=src),
                      writes=[("w", slot)], kind="dma", nobarrier=True)
                wstate["issued"] += 1
            src, a, b = wseq[i]
            slot = i % NSLOT
            return wsl[slot][:, 0:a * b].rearrange("p (a b) -> p a b", b=b), ("w", slot)

        def wcols(wd, l, c0, n):
            return wd[layers.index(l), :, c0:c0 + n].rearrange("(c p) n -> p c n", p=128)

        plan = {}
        for li_, l in enumerate(layers):
            p = {}
            plan[l] = p
            p["kb"] = [wreq(wcols(w_in_d, l, C_KB + 256 * i, 256), 16, 256) for i in range(4)]
            p["vb"] = [wreq(wcols(w_in_d, l, C_VB + 256 * i, 256), 16, 256) for i in range(4)]
            p["qa"] = [wreq(wcols(w_in_d, l, C_QA + 256 * i, 256), 16, 256) for i in range(2)]
            p["ka"] = [wreq(wcols(w_in_d, l, C_KA + 256 * i, 256), 16, 256) for i in range(2)]
            p["va"] = [wreq(wcols(w_in_d, l, C_VA + 256 * i, 256), 16, 256) for i in range(4)]
            p["qb"] = [wreq(wcols(w_in_d, l, C_QB + 256 * i, 256), 16, 256) for i in range(4)]
            p["ra"] = [wreq(wcols(w_in_d, l, C_RA + 256 * i, 256), 16, 256) for i in range(4)]
            if upto <= 4:
                continue
            p["wo"] = [wreq(wcols(w_out_d, l, 256 * i, 256), 16, 256) for i in range(8)]
            p["ffn"] = []
            if upto <= 5:
                continue
            for (j0, nj) in FFN_PARTS:
                gu = []
                for s in range(nj // 2):
                    c0 = (j0 + 2 * s) * 128
                    gu.append((wreq(wcols(w_gate_d, l, c0, 256), 16, 256), wreq(wcols(w_up_d, l, c0, 256), 16, 256)))
                dn = [wreq(w_down_d[li_, j0 * 128:(j0 + nj) * 128, n * 128:(n + 1) * 128]
                           .rearrange("(j p) n -> p j n", p=128), nj, 128) for n in range(16)]
                p["ffn"].append((gu, dn))

        prot = Rot([0, 1])
        srot = Rot([4, 5, 6, 7])

        def dump(name, ap, reads):
            if name in dbg_d:
                S.add("sp", lambda e: e.dma_start(out=dbg_d[name], in_=ap), reads=reads, kind="dma")

        def rmsnorm(gcol0, tag):
            sq_off = 0
            for t in range(2):
                b = srot.next()
                for c in range(NCH):
                    sq = bf(sq_off + (c % 2) * 256, 512)
                    S.add("act", lambda e, sq=sq, c=c, t=t: e.activation(out=sq, in_=xT[:, c, t * 512:(t + 1) * 512], func=AF.Square),
                          reads=[("x", c, t)], writes=[("nsq", c % 2)])
                    S.add("pe", lambda e, sq=sq, c=c, b=b: e.matmul(bank(b), lhsT=ones_b, rhs=sq, start=(c == 0), stop=(c == NCH - 1)),
                          reads=[("nsq", c % 2)] + CB, writes=pk(b))
                rstd = ff(sq_off + 512 + t * 512, 512)
                S.add("act", lambda e, rstd=rstd, b=b: e.activation(out=rstd, in_=bank(b), func=AF.Sqrt, bias=float(D * EPS)),
                      reads=pk(b), writes=[("nrstd", t)])
                S.add("dve", lambda e, rstd=rstd: e.reciprocal(out=rstd, in_=rstd), reads=[("nrstd", t)], writes=[("nrstd", t)])
                for c in range(NCH):
                    S.add("dve", lambda e, rstd=rstd, c=c, t=t: e.scalar_tensor_tensor(
                        out=hT[:, c, t * 512:(t + 1) * 512], in0=xT[:, c, t * 512:(t + 1) * 512],
                        scalar=gsc[:, gcol0 + c:gcol0 + c + 1], in1=rstd, op0=ALU.mult, op1=ALU.mult),
                        reads=[("x", c, t), ("nrstd", t)] + GS, writes=[("h", c, t)])

        def norm_stats_chunk(c, scr_off, extra):
            for t in range(2):
                i = (2 * c + t) % 2
                sq = bf(scr_off + i * 256, 512)
                S.add("act", lambda e, sq=sq, c=c, t=t: e.activation(out=sq, in_=xT[:, c, t * 512:(t + 1) * 512], func=AF.Square),
                      reads=[("x", c, t)], writes=[("nsq", i)] + extra)
                S.add("pe", lambda e, sq=sq, c=c, t=t: e.matmul(bank(6 + t), lhsT=ones_b, rhs=sq, start=(c == 0), stop=(c == NCH - 1)),
                      reads=[("nsq", i)] + CB, writes=pk(6 + t))

        def norm_finish(gcol0):
            for t in range(2):
                rstd = ff(512 + t * 512, 512)
                S.add("act", lambda e, rstd=rstd, t=t: e.activation(out=rstd, in_=bank(6 + t), func=AF.Sqrt, bias=float(D * EPS)),
                      reads=pk(6 + t), writes=[("nrstd", t)])
                S.add("dve", lambda e, rstd=rstd: e.reciprocal(out=rstd, in_=rstd), reads=[("nrstd", t)], writes=[("nrstd", t)])
                for c in range(NCH):
                    S.add("dve", lambda e, rstd=rstd, c=c, t=t: e.scalar_tensor_tensor(
                        out=hT[:, c, t * 512:(t + 1) * 512], in0=xT[:, c, t * 512:(t + 1) * 512],
                        scalar=gsc[:, gcol0 + c:gcol0 + c + 1], in1=rstd, op0=ALU.mult, op1=ALU.mult),
                        reads=[("x", c, t), ("nrstd", t)] + GS, writes=[("h", c, t)])

        HALL = [[("h", c, t) for c in range(NCH)] for t in range(2)]

        def proj_fm(wv, wkey, col0, pair):
            for t in range(2):
                def fn(e, t=t):
                    ins = None
                    for k in range(NCH):
                        ins = e.matmul(ps[pair][:, t * 512:(t + 1) * 512], lhsT=wv[:, k, col0:col0 + 128],
                                       rhs=hT[:, k, t * 512:(t + 1) * 512], start=(k == 0), stop=(k == NCH - 1))
                    return ins
                S.add("pe", fn, reads=[wkey] + HALL[t], writes=pk(2 * pair + t))

        def headnorm_fm(pair, nparts_eps, gcol, out_bf, okey, sq_off, sbanks=None):
            sq = bf(sq_off, 1024)
            S.add("act", lambda e: e.activation(out=sq, in_=ps[pair][:, :], func=AF.Square),
                  reads=pk(2 * pair, 2 * pair + 1), writes=["hn_sq"])
            b0, b1 = sbanks if sbanks is not None else (srot.next(), srot.next())
            for t, b in ((0, b0), (1, b1)):
                S.add("pe", lambda e, t=t, b=b: e.matmul(bank(b), lhsT=ones_b, rhs=sq[:, t * 512:(t + 1) * 512], start=True, stop=True),
                      reads=["hn_sq"] + CB, writes=pk(b))
            rstd = ff(sq_off + 512, 1024)
            for t, b in ((0, b0), (1, b1)):
                S.add("act", lambda e, t=t, b=b: e.activation(out=rstd[:, t * 512:(t + 1) * 512], in_=bank(b), func=AF.Sqrt, bias=float(nparts_eps)),
                      reads=pk(b), writes=[("hn_rstd", t)])
                S.add("dve", lambda e, t=t: e.reciprocal(out=rstd[:, t * 512:(t + 1) * 512], in_=rstd[:, t * 512:(t + 1) * 512]),
                      reads=[("hn_rstd", t)], writes=[("hn_rstd", t)])
            S.add("dve", lambda e: e.scalar_tensor_tensor(out=out_bf, in0=ps[pair][:, :], scalar=gsc[:, gcol:gcol + 1], in1=rstd,
                                                          op0=ALU.mult, op1=ALU.mult),
                  reads=pk(2 * pair, 2 * pair + 1) + [("hn_rstd", 0), ("hn_rstd", 1)] + GS, writes=[okey])

        def mix(k):
            off = 4096 + k * 512 if k < 8 else (k - 8) * 512
            return bf(off, 1024)

        prot4 = Rot([0, 1, 2, 3])
        srot8 = Rot([0, 1, 2, 3, 4, 5, 6, 7])
        srot6 = Rot([2, 3, 4, 5, 6, 7])

        try:
            def do_layer(li, l, pre_normed, fuse_next):
                P = plan[l]
                KL, VL, KA, VA = kloc[l], vloc[l], kall[l], vall[l]
                if pre_normed:
                    norm_finish(l * 16)
                else:
                    rmsnorm(l * 16, "n1")
                if li == 0:
                    dump("h", hT[:, 3, :], HALL[0] + HALL[1])
                if stop == 0:
                    raise _Stop()
                S.barrier()
                kst = [bf(2048 + i * 512, 1024) for i in range(2)]
                vst = [bf(3072 + i * 128, 256) for i in range(4)]
                kvkeys = []
                for s4 in range(4):
                    wv, wkey = wget(P["kb"][s4])
                    for hh in range(2):
                        h = 2 * s4 + hh
                        pair = prot.next()
                        proj_fm(wv, wkey, hh * 128, pair)
                        headnorm_fm(pair, 128 * EPS, 70 + l, kst[h % 2], ("kst", h % 2), 0)
                        S.add("sp", lambda e, h=h: e.dma_start(out=KL.ap()[h * 128:(h + 1) * 128, :], in_=kst[h % 2]),
                              reads=[("kst", h % 2)], writes=[("kvloc", l, "k", h)], kind="dma")
                        kvkeys.append(("kvloc", l, "k", h))
                if stop == 0.3:
                    raise _Stop()
                vi = 0
                for s4 in range(4):
                    wv, wkey = wget(P["vb"][s4])
                    for tb in range(8):
                        b = srot.next()

                        def fn(e, tb=tb, b=b, wv=wv):
                            ins = None
                            for k in range(NCH):
                                ins = e.matmul(bank(b)[:, 0:256], lhsT=hT[:, k, tb * 128:(tb + 1) * 128], rhs=wv[:, k, :],
                                               start=(k == 0), stop=(k == NCH - 1))
                            return ins
                        S.add("pe", fn, reads=[wkey] + HALL[tb // 4], writes=pk(b))
                        vs = vst[vi % 4]
                        S.add("act", lambda e, vs=vs, b=b: e.copy(out=vs, in_=bank(b)[:, 0:256]), reads=pk(b), writes=[("vst", vi % 4)])
                        S.add("sp", lambda e, vs=vs, tb=tb, s4=s4: e.dma_start(
                            out=VL.ap()[tb * 128:(tb + 1) * 128, s4 * 256:(s4 + 1) * 256], in_=vs),
                            reads=[("vst", vi % 4)], writes=[("kvloc", l, "v", s4, tb)], kind="dma")
                        kvkeys.append(("kvloc", l, "v", s4, tb))
                        vi += 1
                if stop == 0.6:
                    raise _Stop()
                S.add("pool", lambda e: e.collective_compute("AllGather", ALU.bypass, replica_groups=[[0, 1], [2, 3], [4, 5], [6, 7]],
                                                             ins=[KL.ap().opt()], outs=[KA.ap().opt()]),
                      reads=kvkeys, writes=[("kall", l)], kind="cc", nobarrier=True)
                S.add("pool", lambda e: e.collective_compute("AllGather", ALU.bypass, replica_groups=[[0, 1], [2, 3], [4, 5], [6, 7]],
                                                             ins=[VL.ap().opt()], outs=[VA.ap().opt()]),
                      reads=kvkeys, writes=[("vall", l)], kind="cc", nobarrier=True)
                if stop == 1:
                    raise _Stop()
                S.barrier()

                qT = [bf(512 * h, 1024) for h in range(4)]
                kT = [bf(2048 + 512 * h, 1024) for h in range(4)]
                vtok = bf(4096, 8192).rearrange("p (a b) -> p a b", b=1024)
                lp = ff(8192, 4096).rearrange("p (a b) -> p a b", b=512)
                g1T = BIG[0:16, 12288:13312]
                E = ff(13312, 1024)
                qcst = [bf(14336 + 512 * i, 1024) for i in range(2)]
                dect = ff(15360, 32)
                cdt = ff(15392, 32)
                w2l = BIG[0:32, 15424:15936]
                g1T32 = BIG[0:32, 12288:13312]
                S.add("dve", lambda e: e.memset(w2l, 0.0), writes=["w2", "gb"])
                S.add("dve", lambda e: e.memset(g1T32, 1.0), writes=["g1T"])
                S.add("sp", lambda e: e.dma_start(out=BIG[0:16, 15424:15936], in_=w2_d[:, l * 512:(l + 1) * 512]), writes=["w2"], kind="dma")
                S.add("sp", lambda e: e.dma_start(out=BIG[16:17, 15424:15936], in_=gb_d[:, l * 512:(l + 1) * 512]), writes=["gb"], kind="dma")
                for name, plist, dst in (("qa", P["qa"], qT), ("ka", P["ka"], kT)):
                    for s2 in range(2):
                        wv, wkey = wget(plist[s2])
                        for hh in range(2):
                            h = 2 * s2 + hh
                            pair = prot.next()
                            proj_fm(wv, wkey, hh * 128, pair)
                            S.add("act", lambda e, pair=pair, d=dst[h]: e.copy(out=d, in_=ps[pair][:, :]),
                                  reads=pk(2 * pair, 2 * pair + 1), writes=[(name, h)])
                if stop == 1.2:
                    raise _Stop()
                for s4 in range(4):
                    wv, wkey = wget(P["va"][s4])
                    for tb in range(8):
                        b = srot.next()

                        def fn(e, tb=tb, b=b, wv=wv):
                            ins = None
                            for k in range(NCH):
                                ins = e.matmul(bank(b)[:, 0:256], lhsT=hT[:, k, tb * 128:(tb + 1) * 128], rhs=wv[:, k, :],
                                               start=(k == 0), stop=(k == NCH - 1))
                            return ins
                        S.add("pe", fn, reads=[wkey] + HALL[tb // 4], writes=pk(b))
                        S.add("act", lambda e, tb=tb, b=b, s4=s4: e.copy(out=vtok[:, tb, s4 * 256:(s4 + 1) * 256], in_=bank(b)[:, 0:256]),
                              reads=pk(b), writes=[("va", tb, s4)])
                if stop == 1.4:
                    raise _Stop()
                pair = prot.next()
                wgl = wga[:, l * 256:(l + 1) * 256].rearrange("p (c n) -> p c n", n=16)
                for t in range(2):
                    def fn(e, t=t, pair=pair, wgl=wgl):
                        ins = None
                        for k in range(NCH):
                            ins = e.matmul(ps[pair][0:16, t * 512:(t + 1) * 512], lhsT=wgl[:, k, :], rhs=hT[:, k, t * 512:(t + 1) * 512],
                                           start=(k == 0), stop=(k == NCH - 1))
                        return ins
                    S.add("pe", fn, reads=[("wga", l)] + HALL[t], writes=pk(2 * pair + t))
                S.add("act", lambda e, pair=pair: e.copy(out=g1T, in_=ps[pair][0:16, :]), reads=pk(2 * pair, 2 * pair + 1), writes=["g1T"])
                if stop == 1.5:
                    raise _Stop()
                for tb in range(8):
                    b = srot.next()

                    def fn(e, tb=tb, b=b):
                        return e.matmul(bank(b), lhsT=g1T32[:, tb * 128:(tb + 1) * 128], rhs=w2l, start=True, stop=True)
                    S.add("pe", fn, reads=["g1T", "w2", "gb", "cst"], writes=pk(b))
                    S.add("act", lambda e, tb=tb, b=b: e.activation(out=lp[:, tb, :], in_=bank(b), func=AF.Exp, scale=-1.0),
                          reads=pk(b), writes=[("lp", tb)])
                    S.add("act", lambda e, tb=tb: e.activation(out=lp[:, tb, :], in_=lp[:, tb, :], func=AF.Ln, bias=1.0),
                          reads=[("lp", tb)], writes=[("lp", tb)])
                if stop == 1.6:
                    raise _Stop()
                if li == 0:
                    dump("lp", lp[:, 0, :], [("lp", 0)])
                for h in range(4):
                    pair = prot.next()
                    for n in range(8):
                        S.add("pe", lambda e, n=n, pair=pair, h=h: e.matmul(ps[pair][:, n * 128:(n + 1) * 128], lhsT=lp[:, n, h * 128:(h + 1) * 128],
                                                                            rhs=tri_f, start=True, stop=True),
                              reads=[("lp", n), "cst"], writes=pk(2 * pair + n // 4))
                    PP = pk(2 * pair, 2 * pair + 1)
                    S.add("act", lambda e, pair=pair: e.activation(out=E, in_=ps[pair][:, :], func=AF.Exp), reads=PP, writes=["E"])
                    S.add("dve", lambda e, h=h: e.tensor_copy(out=dect[:, h * 8:(h + 1) * 8], in_=E.rearrange("p (a b) -> p a b", b=128)[:, :, 127]),
                          reads=["E"], writes=[("dec", h)])
                    S.add("dve", lambda e, h=h: e.memset(cdt[:, h * 8:h * 8 + 1], 1.0), writes=[("cd", h)])
                    for n in range(1, 8):
                        S.add("dve", lambda e, n=n, h=h: e.tensor_tensor(out=cdt[:, h * 8 + n:h * 8 + n + 1], in0=cdt[:, h * 8 + n - 1:h * 8 + n],
                                                                        in1=dect[:, h * 8 + n - 1:h * 8 + n], op=ALU.mult),
                              reads=[("cd", h), ("dec", h)], writes=[("cd", h)])
                    S.add("dve", lambda e, h=h: e.scalar_tensor_tensor(out=qT[h], in0=qT[h], scalar=float(128.0 ** -0.5), in1=E,
                                                                       op0=ALU.mult, op1=ALU.mult),
                          reads=[("qa", h), "E"], writes=[("qa", h)])
                    qc = qcst[h % 2]
                    S.add("dve", lambda e, h=h, qc=qc: e.tensor_tensor(
                        out=qc.rearrange("p (a b) -> p a b", b=128), in0=qT[h].rearrange("p (a b) -> p a b", b=128),
                        in1=cdt[:, h * 8:(h + 1) * 8].unsqueeze(2).to_broadcast([128, 8, 128]), op=ALU.mult),
                        reads=[("cd", h), ("qa", h)], writes=[("qcst", h % 2)])
                    S.add("sp", lambda e, h=h, qc=qc: e.dma_start(out=park_q[l].ap()[h], in_=qc), reads=[("qcst", h % 2)],
                          writes=[("parkq", l, h)], kind="dma")
                    S.add("act", lambda e, pair=pair: e.activation(out=E, in_=ps[pair][:, :], func=AF.Exp, scale=-1.0), reads=PP, writes=["E"])
                    S.add("dve", lambda e, h=h: e.tensor_tensor(out=kT[h], in0=kT[h], in1=E, op=ALU.mult),
                          reads=[("ka", h), "E"], writes=[("ka", h)])
                if li == 0:
                    dump("qt", qT[1], [("qa", 1)])
                    dump("kt", kT[1], [("ka", 1)])
                if stop == 2:
                    raise _Stop()
                S.barrier()
                ktok = [bf(8192 + 512 * h, 1024).rearrange("p (a b) -> p a b", b=128) for h in range(4)]
                S_f = [[ff(10240 + 512 * h + 256 * i, 256) for i in range(2)] for h in range(4)]
                S_b = [[bf(12288 + 256 * h + 128 * i, 256) for i in range(2)] for h in range(4)]
                am = [bf(13312 + 64 * i, 128) for i in range(4)]
                ost = [ff(13568 + 256 * i, 256) for i in range(4)]
                sfkeys = []
                for h in range(4):
                    kt_ = ktok[h]
                    b = srot8.next()
                    pbv = bank(b).bitcast(BF16)
                    for n in range(8):
                        S.add("pe", lambda e, n=n, pbv=pbv, h=h: e.transpose(pbv[:, n * 128:(n + 1) * 128], kT[h][:, n * 128:(n + 1) * 128], ident_b),
                              reads=[("ka", h)] + CB, writes=pk(b))
                    S.add("act", lambda e, pbv=pbv, kt_=kt_: e.copy(out=kt_, in_=pbv.rearrange("p (a b) -> p a b", b=128)),
                          reads=pk(b), writes=[("ktok", h)])
                ci = 0
                for n in range(8):
                    for h in range(4):
                        kt_ = ktok[h]
                        cs_ = slice(n * 128, (n + 1) * 128)
                        ba, bo, bc = srot8.next(), srot8.next(), srot8.next()
                        a_ = am[ci % 4]
                        o_ = ost[ci % 4]
                        Sn, So = S_f[h][n % 2], S_f[h][(n + 1) % 2]
                        Sbn, Sbo = S_b[h][n % 2], S_b[h][(n + 1) % 2]
                        kSn, kSo = ("Sf", h, n % 2), ("Sf", h, (n + 1) % 2)
                        kBn, kBo = ("Sb", h, n % 2), ("Sb", h, (n + 1) % 2)
                        S.add("pe", lambda e, ba=ba, h=h, cs_=cs_: e.matmul(bank(ba)[:, 0:128], lhsT=kT[h][:, cs_], rhs=qT[h][:, cs_], start=True, stop=True),
                              reads=[("ka", h), ("qa", h)], writes=pk(ba))
                        S.add("dve", lambda e, ba=ba, a_=a_: e.tensor_tensor(out=a_, in0=bank(ba)[:, 0:128], in1=caus_f, op=ALU.mult),
                              reads=pk(ba) + ["cst"], writes=[("am", ci % 4)])

                        def fo(e, bo=bo, n=n, h=h, a_=a_, Sbo=Sbo, cs_=cs_):
                            ins = None
                            for vc in range(2):
                                ins = e.matmul(bank(bo)[:, vc * 128:(vc + 1) * 128], lhsT=vtok[:, n, h * 256 + vc * 128:h * 256 + (vc + 1) * 128],
                                               rhs=a_, start=True, stop=(n == 0))
                                if n > 0:
                                    ins = e.matmul(bank(bo)[:, vc * 128:(vc + 1) * 128], lhsT=Sbo[:, vc * 128:(vc + 1) * 128],
                                                   rhs=qT[h][:, cs_], start=False, stop=True)
                            return ins
                        S.add("pe", fo, reads=[("am", ci % 4), ("qa", h)] + [("va", n, s4) for s4 in range(4)] + ([kBo] if n > 0 else []),
                              writes=pk(bo))
                        S.add("act", lambda e, bo=bo, o_=o_: e.copy(out=o_, in_=bank(bo)[:, 0:256]), reads=pk(bo), writes=[("ost", ci % 4)])
                        S.add("sp", lambda e, o_=o_, h=h, cs_=cs_: e.dma_start(
                            out=park_o[l].ap()[h].rearrange("p (a t) -> p a t", a=2)[:, :, cs_], in_=o_.rearrange("p (a b) -> p a b", a=2)),
                            reads=[("ost", ci % 4)], writes=[("parko", l, h, n)], kind="dma")
                        S.add("pe", lambda e, bc=bc, kt_=kt_, n=n, h=h: e.matmul(bank(bc)[:, 0:256], lhsT=kt_[:, n, :], rhs=vtok[:, n, h * 256:(h + 1) * 256],
                                                                               start=True, stop=True),
                              reads=[("ktok", h)] + [("va", n, s4) for s4 in range(4)], writes=pk(bc))
                        dcol = dect[:, h * 8 + n:h * 8 + n + 1]
                        if n == 0:
                            S.add("dve", lambda e, bc=bc, Sn=Sn, dcol=dcol: e.tensor_scalar(out=Sn, in0=bank(bc)[:, 0:256], scalar1=dcol, scalar2=None, op0=ALU.mult),
                                  reads=pk(bc) + [("dec", h)], writes=[kSn])
                        else:
                            S.add("dve", lambda e, bc=bc, Sn=Sn, So=So: e.tensor_tensor(out=Sn, in0=bank(bc)[:, 0:256], in1=So, op=ALU.add),
                                  reads=pk(bc) + [kSo], writes=[kSn])
                            S.add("dve", lambda e, Sn=Sn, dcol=dcol: e.tensor_scalar(out=Sn, in0=Sn, scalar1=dcol, scalar2=None, op0=ALU.mult),
                                  reads=[kSn, ("dec", h)], writes=[kSn])
                        if n < 7:
                            S.add("act", lambda e, Sn=Sn, Sbn=Sbn: e.copy(out=Sbn, in_=Sn), reads=[kSn], writes=[kBn])
                        else:
                            S.add("sp", lambda e, Sn=Sn, h=h: e.dma_start(out=sfl[l].ap()[h * 128:(h + 1) * 128, :], in_=Sn),
                                  reads=[kSn], writes=[("sfl", l, h)], kind="dma")
                            sfkeys.append(("sfl", l, h))
                        ci += 1
                S.add("pool", lambda e: e.collective_compute("AllGather", ALU.bypass, replica_groups=[[0, 1], [2, 3], [4, 5], [6, 7]],
                                                             ins=[sfl[l].ap().opt()], outs=[sfa[l].ap().opt()]),
                      reads=sfkeys, writes=[("sfa", l)], kind="cc", nobarrier=True)
                if stop == 3:
                    raise _Stop()
                S.barrier()

                qn = bf(5632, 1024)
                BMf = ff(9536, 832)
                ptb = [bf(10784 + 256 * i, 512) for i in range(3)]
                rsb = ff(11552, 1024)

                def hbufs(s_):
                    if s_ == 0:
                        o = dict(K=6144, V1=7168, V2=7744, V3=8512, BMb=10368)
                    else:
                        o = dict(K=12576, V1=13600, V2=14176, V3=14944, BMb=15968)
                    return dict(Kall=bf(o["K"], 2048),
                                V1=bf(o["V1"], 1152).rearrange("p (a b) -> p a b", b=128),
                                V2=bf(o["V2"], 1536).rearrange("p (a r b) -> p a r b", a=3, r=4),
                                V3=bf(o["V3"], 2048).rearrange("p (a b) -> p a b", b=128),
                                BMb=bf(o["BMb"], 832))
                HB = [hbufs(0), hbufs(1)]
                KAa, VAa, KLa, VLa = KA.ap(), VA.ap(), KL.ap(), VL.ap()

                def head_loads(h):
                    s_ = h % 2
                    B_ = HB[s_]
                    hc = slice(h * 128, (h + 1) * 128)
                    Kall, V1, V2, V3 = B_["Kall"], B_["V1"], B_["V2"], B_["V3"]
                    vkeys = [("kvloc", l, "v", h // 2, tb) for tb in range(8)]
                    ld = lambda fn, reads, key: S.add("sp", fn, reads=reads, writes=[(key, s_)], kind="dma")
                    ld(lambda e: e.dma_start(out=Kall[:, 0:1024], in_=KAa[h * 128:(h + 1) * 128, :]), [("kall", l)], "Kall0")
                    ld(lambda e: e.dma_start(out=Kall[:, 1024:2048], in_=KLa[h * 128:(h + 1) * 128, :]), [("kvloc", l, "k", h)], "Kall1")
                    ld(lambda e: e.dma_start(out=V1[:, 0, :], in_=VAa[896:1024, hc]), [("vall", l)], "V1p")
                    ld(lambda e: e.dma_start(out=V1[:, 1:9, :], in_=VLa[0:1024, hc].rearrange("(a p) d -> p a d", p=128)), vkeys, "V1l")
                    ld(lambda e: e.dma_start(out=V2[:, 0, :, :], in_=VAa[512:1024, hc].rearrange("(i r) d -> i r d", r=4)), [("vall", l)], "V2p")
                    ld(lambda e: e.dma_start(out=V2[:, 1, :, :], in_=VLa[0:512, hc].rearrange("(i r) d -> i r d", r=4)), vkeys, "V2a")
                    ld(lambda e: e.dma_start(out=V2[:, 2, :, :], in_=VLa[512:1024, hc].rearrange("(i r) d -> i r d", r=4)), vkeys, "V2b")
                    ld(lambda e: e.dma_start(out=V3[0:64, :, :], in_=VAa[0:1024, hc].rearrange("(j r) d -> j r d", r=16)), [("vall", l)], "V3p")
                    ld(lambda e: e.dma_start(out=V3[64:128, :, :], in_=VLa[0:1024, hc].rearrange("(j r) d -> j r d", r=16)), vkeys, "V3l")
                    S.add("sp", lambda e: e.dma_start(out=BMf, in_=bias_d[h]), writes=["BMf"], kind="dma")
                    S.add("dve", lambda e: e.tensor_tensor(out=B_["BMb"], in0=BMf, in1=msk[:, :], op=ALU.add), reads=["BMf", "msk"], writes=[("BMb", s_)])

                lrot = Rot([4, 5, 6, 7])
                pti = [0]
                head_loads(0)
                for h in range(8):
                    s_ = h % 2
                    B_ = HB[s_]
                    Kall, V1, V2, V3, BMb = B_["Kall"], B_["V1"], B_["V2"], B_["V3"], B_["BMb"]
                    if h % 2 == 0:
                        wv, wkey = wget(P["qb"][h // 2])
                    if h + 1 < 8:
                        head_loads(h + 1)
                    proj_fm(wv, wkey, (h % 2) * 128, 3)
                    headnorm_fm(3, 128 * EPS, 68 + l, qn, "qn", 4096, sbanks=(4, 5))
                    KK = [("Kall0", s_), ("Kall1", s_), ("BMb", s_), "qn"] + CB
                    first = {0: True, 1: True, 2: True, 3: True}
                    groups = []
                    items = []
                    for kbi in range(9):
                        keys = Kall[:, 896 + kbi * 128:896 + (kbi + 1) * 128]
                        if kbi == 0:
                            q0, q1, bm = 0, 128, BMb[:, 576:704]
                        elif kbi == 8:
                            q0, q1, bm = 896, 1024, BMb[:, 0:128]
                        else:
                            q0, q1, bm = (kbi - 1) * 128, (kbi + 1) * 128, BMb[:, 0:256]
                        pieces = []
                        a = q0
                        while a < q1:
                            b_ = min(q1, (a // 512 + 1) * 512)
                            pieces.append((a // 512, (lambda bk, a=a, b_=b_: bk[:, a % 512:(b_ - 1) % 512 + 1]), a - q0, b_ - a))
                            a = b_
                        items.append((keys, qn[:, q0:q1], bm, q1 - q0, None, V1[:, kbi, :], [("V1p", s_), ("V1l", s_)], pieces))
                    groups += [items[0:2], items[2:4], items[4:6], items[6:8], items[8:9]]
                    qn4 = qn.rearrange("p (n i r) -> p n i r", n=2, r=4)
                    K4 = Kall.rearrange("p (n i r) -> p n i r", n=4, r=4)
                    for r in range(4):
                        colf = lambda bk, r=r: bk.rearrange("p (i r) -> p i r", r=4)[:, :, r]
                        g = [(K4[:, 1, :, r], qn4[:, 0, :, r], BMb[:, 704:832], 128, None, V2[:, 0, r, :], [("V2p", s_)], [(0, colf, 0, 128)]),
                             (K4[:, 2, :, r], qn4[:, :, :, r], BMb[:, 256:512], 256, 128, V2[:, 1, r, :], [("V2a", s_)], [(0, colf, 0, 128), (1, colf, 128, 128)]),
                             (K4[:, 3, :, r], qn4[:, 1, :, r], BMb[:, 256:384], 128, None, V2[:, 2, r, :], [("V2b", s_)], [(1, colf, 0, 128)])]
                        groups.append(g)
                    qn16 = qn.rearrange("p (i r) -> p i r", r=16)
                    K16 = Kall.rearrange("p (j r) -> p j r", r=16)
                    for r0 in (0, 8):
                        g = []
                        for r in range(r0, r0 + 8):
                            colf = lambda bk, r=r: bk.rearrange("p (i r) -> p i r", r=16)[:, :, r]
                            g.append((K16[:, :, r], qn16[:, :, r], BMb[:, 512:576], 64, None, V3[:, r, :], [("V3p", s_), ("V3l", s_)],
                                      [(0, colf, 0, 32), (1, colf, 32, 32)]))
                        groups.append(g)
                    def rec_log(g):
                        bl = lrot.next()
                        W_ = sum(it[3] for it in g)
                        pt = ptb[pti[0] % 3]
                        ptk = ("ptb", pti[0] % 3)
                        pti[0] += 1

                        def f_log(e, g=g, bl=bl):
                            off = 0
                            ins = None
                            for (keys, q, bm, n_, sp_, v, vk, pieces) in g:
                                o_ = bank(bl)[:, off:off + n_]
                                if sp_ is not None:
                                    o_ = o_.rearrange("p (a b) -> p a b", b=sp_)
                                e.matmul(o_, lhsT=keys, rhs=q, start=True, stop=False)
                                ins = e.matmul(bank(bl)[:, off:off + n_], lhsT=ident_b, rhs=bm, start=False, stop=True)
                                off += n_
                            return ins
                        S.add("pe", f_log, reads=KK, writes=pk(bl))
                        S.add("act", lambda e, bl=bl, W_=W_, pt=pt: e.activation(out=pt[:, 0:W_], in_=bank(bl)[:, 0:W_], func=AF.Exp),
                              reads=pk(bl), writes=[ptk])
                        return (g, pt, ptk)

                    def rec_pv(ctx):
                        g, pt, ptk = ctx
                        flags = []
                        for it in g:
                            for (b01, colf, lo, n_) in it[7]:
                                flags.append((first[b01], first[b01 + 2]))
                                first[b01] = False
                                first[b01 + 2] = False

                        def f_pv(e, g=g, pt=pt, flags=flags):
                            off = 0
                            ins = None
                            fi = 0
                            for (keys, q, bm, n_, sp_, v, vk, pieces) in g:
                                for (b01, colf, lo, pn) in pieces:
                                    st_o, st_s = flags[fi]
                                    fi += 1
                                    e.matmul(_piece(colf, bank(b01), pn), lhsT=v,
                                             rhs=pt[:, off + lo:off + lo + pn], start=st_o, stop=True, skip_group_check=True)
                                    ins = e.matmul(_piece(colf, bank(b01 + 2), pn), lhsT=ones_b, rhs=pt[:, off + lo:off + lo + pn],
                                                   start=st_s, stop=True, skip_group_check=True)
                                off += n_
                            return ins
                        vks = []
                        for it in g:
                            vks += it[6]
                        S.add("pe", f_pv, reads=[ptk] + vks + CB, writes=pk(0, 1, 2, 3))

                    pend = []
                    for g in groups:
                        pend.append(rec_log(g))
                        if len(pend) > 2:
                            rec_pv(pend.pop(0))
                    while pend:
                        rec_pv(pend.pop(0))
                    S.add("dve", lambda e: e.reciprocal(out=rsb, in_=ps[1][:, :]), reads=pk(2, 3), writes=["rsb"])
                    S.add("dve", lambda e, h=h: e.tensor_tensor(out=mix(8 + h), in0=ps[0][:, :], in1=rsb, op=ALU.mult),
                          reads=pk(0, 1) + ["rsb"], writes=[("mix", 8 + h)])
                if li == 0:
                    dump("mixb", mix(9), [("mix", 9)])
                if stop == 4:
                    raise _Stop()
                S.barrier()

                sr = ff(8192, 2048).rearrange("p (a b) -> p a b", b=1024)
                osb = ff(10240, 2048).rearrange("p (a b) -> p a b", b=1024)
                qcl = bf(12288, 1024)
                Smf = ff(12800, 256)
                Smb = bf(13056, 256)
                sq4 = bf(13184, 2048).rearrange("p (a b) -> p a b", b=1024)
                rstd4 = ff(14208, 1024)
                t4 = ff(15232, 1024)
                for h in range(4):
                    wv, wkey = wget(P["ra"][h])
                    S.add("sp", lambda e, h=h: e.dma_start(out=osb, in_=park_o[l].ap()[h].rearrange("p (a t) -> p a t", a=2)),
                          reads=[("parko", l, h, n) for n in range(8)], writes=[("osb", 0), ("osb", 1)], kind="dma")
                    S.add("sp", lambda e, h=h: e.dma_start(out=qcl, in_=park_q[l].ap()[h]), reads=[("parkq", l, h)], writes=["qcl"], kind="dma")
                    S.add("sp", lambda e, h=h: e.dma_start(out=Smf, in_=sfa[l].ap()[h * 128:(h + 1) * 128, :]), reads=[("sfa", l)], writes=["Smf"], kind="dma")
                    S.add("dve", lambda e: e.tensor_scalar(out=Smb, in0=Smf, scalar1=flag, scalar2=None, op0=ALU.mult), reads=["Smf", "gpar"], writes=["Smb"])
                    for vc in range(2):
                        pair = prot4.next()
                        proj_fm(wv, wkey, vc * 128, pair)
                        S.add("act", lambda e, vc=vc, pair=pair: e.activation(out=sr[:, vc, :], in_=ps[pair][:, :], func=AF.Silu),
                              reads=pk(2 * pair, 2 * pair + 1), writes=[("sr", vc)])
                    for vc in range(2):
                        pair = prot4.next()
                        for t in range(2):
                            S.add("pe", lambda e, vc=vc, pair=pair, t=t: e.matmul(ps[pair][:, t * 512:(t + 1) * 512], lhsT=Smb[:, vc * 128:(vc + 1) * 128],
                                                                                  rhs=qcl[:, t * 512:(t + 1) * 512], start=True, stop=True),
                                  reads=["Smb", "qcl"], writes=pk(2 * pair + t))
                        S.add("dve", lambda e, vc=vc, pair=pair: e.tensor_tensor(out=osb[:, vc, :], in0=ps[pair][:, :], in1=osb[:, vc, :], op=ALU.add),
                              reads=pk(2 * pair, 2 * pair + 1) + [("osb", vc)], writes=[("osb", vc)])
                        S.add("act", lambda e, vc=vc: e.activation(out=sq4[:, vc, :], in_=osb[:, vc, :], func=AF.Square), reads=[("osb", vc)], writes=[("sq4", vc)])
                    pair = prot4.next()
                    for t in range(2):
                        def fn(e, t=t, pair=pair):
                            e.matmul(ps[pair][:, t * 512:(t + 1) * 512], lhsT=ones_b, rhs=sq4[:, 0, t * 512:(t + 1) * 512], start=True, stop=False)
                            return e.matmul(ps[pair][:, t * 512:(t + 1) * 512], lhsT=ones_b, rhs=sq4[:, 1, t * 512:(t + 1) * 512], start=False, stop=True)
                        S.add("pe", fn, reads=[("sq4", 0), ("sq4", 1)] + CB, writes=pk(2 * pair + t))
                    S.add("act", lambda e, pair=pair: e.activation(out=rstd4, in_=ps[pair][:, :], func=AF.Sqrt, bias=float(256 * EPS)),
                          reads=pk(2 * pair, 2 * pair + 1), writes=["rstd4"])
                    S.add("dve", lambda e: e.reciprocal(out=rstd4, in_=rstd4), reads=["rstd4"], writes=["rstd4"])
                    for vc in range(2):
                        S.add("dve", lambda e, vc=vc: e.scalar_tensor_tensor(out=t4, in0=osb[:, vc, :], scalar=gsc[:, 64 + l * 2 + vc:64 + l * 2 + vc + 1], in1=rstd4,
                                                                             op0=ALU.mult, op1=ALU.mult),
                              reads=[("osb", vc), "rstd4"] + GS, writes=["t4"])
                        S.add("dve", lambda e, vc=vc, h=h: e.tensor_tensor(out=mix(2 * h + vc), in0=t4, in1=sr[:, vc, :], op=ALU.mult),
                              reads=["t4", ("sr", vc)], writes=[("mix", 2 * h + vc)])
                if li == 0:
                    dump("mixa", mix(1), [("mix", 1)])
                MIXK = [("mix", k) for k in range(16)]
                prot3 = Rot([0, 1, 2])
                pend_n = []
                for s8 in range(8):
                    wv, wkey = wget(P["wo"][s8])
                    for nn in range(2):
                        n = 2 * s8 + nn
                        pair = prot3.next()
                        for t in range(2):
                            def fn(e, t=t, pair=pair, nn=nn, wv=wv):
                                ins = None
                                for k in range(16):
                                    ins = e.matmul(ps[pair][:, t * 512:(t + 1) * 512], lhsT=wv[:, k, nn * 128:(nn + 1) * 128],
                                                   rhs=mix(k)[:, t * 512:(t + 1) * 512], start=(k == 0), stop=(k == 15))
                                return ins
                            S.add("pe", fn, reads=[wkey] + MIXK, writes=pk(2 * pair + t))
                        while len(pend_n) > 1:
                            norm_stats_chunk(pend_n.pop(0), 8192, [("sr", 0)])
                        S.add("dve", lambda e, n=n, pair=pair: e.tensor_tensor(out=xT[:, n, :], in0=ps[pair][:, :], in1=xT[:, n, :], op=ALU.add),
                              reads=pk(2 * pair, 2 * pair + 1) + [("x", n, 0), ("x", n, 1)], writes=[("x", n, 0), ("x", n, 1)])
                        pend_n.append(n)
                while pend_n:
                    norm_stats_chunk(pend_n.pop(0), 8192, [("sr", 0)])
                if li == 0:
                    dump("x1", xT[:, 5, :], [("x", 5, 0), ("x", 5, 1)])
                if stop == 5:
                    raise _Stop()
                S.barrier()
                norm_finish(32 + l * 16)
                if stop == 6:
                    raise _Stop()
                S.barrier()
                aT = bf(0, 24 * 1024).rearrange("p (a b) -> p a b", b=1024)
                sg = [ff(12288 + 1024 * i, 1024) for i in range(2)]
                gi = 0
                for part, (j0, nj) in enumerate(FFN_PARTS):
                    gu, dn = P["ffn"][part]
                    for s in range(nj // 2):
                        wg, kg = wget(gu[s][0])
                        wu, ku = wget(gu[s][1])
                        for jj in range(2):
                            j = 2 * s + jj
                            pg, pu = prot4.next(), prot4.next()
                            proj_fm(wg, kg, jj * 128, pg)
                            proj_fm(wu, ku, jj * 128, pu)
                            sgi = sg[gi % 2]
                            S.add("act", lambda e, pg=pg, sgi=sgi: e.activation(out=sgi, in_=ps[pg][:, :], func=AF.Silu),
                                  reads=pk(2 * pg, 2 * pg + 1), writes=[("sg", gi % 2)])
                            S.add("dve", lambda e, pu=pu, sgi=sgi, j=j: e.tensor_tensor(out=aT[:, j, :], in0=ps[pu][:, :], in1=sgi, op=ALU.mult),
                                  reads=pk(2 * pu, 2 * pu + 1) + [("sg", gi % 2)], writes=[("aT", j)])
                            gi += 1
                    AK = [("aT", j) for j in range(nj)]
                    fuse = fuse_next and part == len(FFN_PARTS) - 1
                    protB = Rot([0, 1, 2]) if fuse else prot4
                    pend_n = []
                    for n in range(16):
                        wd, kd = wget(dn[n])
                        pair = protB.next()

                        def fn(e, pair=pair, wd=wd, nj=nj):
                            ins = None
                            for j in range(nj):
                                for t in range(2):
                                    ins = e.matmul(ps[pair][:, t * 512:(t + 1) * 512], lhsT=wd[:, j, :], rhs=aT[:, j, t * 512:(t + 1) * 512],
                                                   start=(j == 0), stop=(j == nj - 1))
                            return ins
                        S.add("pe", fn, reads=[kd] + AK, writes=pk(2 * pair, 2 * pair + 1))
                        while fuse and len(pend_n) > 1:
                            norm_stats_chunk(pend_n.pop(0), 14336, [])
                        S.add("dve", lambda e, n=n, pair=pair: e.tensor_tensor(out=xT[:, n, :], in0=ps[pair][:, :], in1=xT[:, n, :], op=ALU.add),
                              reads=pk(2 * pair, 2 * pair + 1) + [("x", n, 0), ("x", n, 1)], writes=[("x", n, 0), ("x", n, 1)])
                        pend_n.append(n)
                    while fuse and pend_n:
                        norm_stats_chunk(pend_n.pop(0), 14336, [])
                    S.barrier()
            for li_0, l_0 in enumerate(layers):
                do_layer(li_0, l_0, li_0 > 0, li_0 + 1 < len(layers))
        except _Stop:
            pass
        for q in range(4):
            S.add("sp", lambda e, q=q: e.dma_start(
                out=yT_d[512 * q:512 * (q + 1), :].rearrange("(c p) t -> p c t", p=128), in_=xT[:, 4 * q:4 * q + 4, :]),
                reads=[("x", c, t) for c in range(4 * q, 4 * q + 4) for t in range(2)], kind="dma")
        S.emit()
    return nc


MODE = "fused"
_CACHE = {}


def _get_prog(layers, dbg=()):
    key = (tuple(layers), tuple(dbg))
    if key not in _CACHE:
        _CACHE[key] = build(layers, dbg)
    return _CACHE[key]


def _prep_maps(inputs, xTs, layers=(0, 1), names=None):
    f = lambda k: np.ascontiguousarray(np.asarray(inputs[k], dtype=np.float32))
    fw = lambda k: np.ascontiguousarray(np.asarray(inputs[k], dtype=np.float32)[list(layers)])
    w_in, w_out, w_gate, w_up, w_down = fw("w_in"), fw("w_out"), fw("w_gate"), fw("w_up"), fw("w_down")
    g1 = f("norm1_g").reshape(2, 16, 128).transpose(2, 0, 1).reshape(128, 32)
    g2 = f("norm2_g").reshape(2, 16, 128).transpose(2, 0, 1).reshape(128, 32)
    og = f("gla_onorm_g").reshape(2, 2, 128).transpose(2, 0, 1).reshape(128, 4)
    qg = f("q_norm_g").T
    kg = f("k_norm_g").T
    w2 = np.ascontiguousarray(f("gla_gate_w2").transpose(1, 0, 2).reshape(16, 1024))
    gb = np.ascontiguousarray(f("gla_gate_b").reshape(1, 1024))
    rel_bias = f("rel_bias")
    cst = _consts()
    maps = []
    for c in range(8):
        half = c % 2
        gpar = np.zeros((128, 80), np.float32)
        gpar[:, 0:32], gpar[:, 32:64], gpar[:, 64:68] = g1, g2, og
        gpar[:, 68:70], gpar[:, 70:72] = qg, kg
        gpar[:, 72] = float(half)
        idx, msk = _bias_index_and_mask(half)
        biasT = np.ascontiguousarray(rel_bias[idx, :].transpose(2, 0, 1))
        m = {"xT": xTs[c], "w_in": w_in, "w_out": w_out, "w_gate": w_gate, "w_up": w_up, "w_down": w_down,
             "gpar": gpar, "w2": w2, "gb": gb, "cst": cst, "biasT": biasT, "maskT": msk}
        if names is not None:
            m = {k: v for k, v in m.items() if k in names}
        maps.append(m)
    return maps


def kernel(**inputs):
    x = np.asarray(inputs["x"], dtype=np.float32)
    xTs = [np.ascontiguousarray(x[c // 2, (c % 2) * T:(c % 2 + 1) * T, :].T) for c in range(8)]
    groups = [(0, 1)] if MODE == "fused" else [(0,), (1,)]
    for layers in groups:
        nc = _get_prog(layers)
        maps = _prep_maps(inputs, xTs, layers)
        res = run_bass_kernel_spmd(nc, maps, core_ids=list(range(8)))
        xTs = [np.ascontiguousarray(res.results[c]["yT"]) for c in range(8)]
    out = np.empty_like(x)
    for c in range(8):
        out[c // 2, (c % 2) * T:(c % 2 + 1) * T, :] = xTs[c].T
    return out
```

```python
import numpy as np
import concourse.bass as bass
import concourse.mybir as mybir
from concourse.bass_utils import run_bass_kernel_spmd

F32 = mybir.dt.float32
BF16 = mybir.dt.bfloat16
AF = mybir.ActivationFunctionType
ALU = mybir.AluOpType

D = 2048
T = 1024
NCH = 16
N_IN = 6160
FFN = 5632
NJ = 44
EPS = 1e-6
NEG = -200.0
C_QA, C_KA, C_VA, C_RA, C_GA, C_QB, C_KB, C_VB = 0, 512, 1024, 2048, 3072, 3088, 4112, 5136
NSLOT = 5
NS_DMA = 8
FFN_PARTS = ((0, 24), (24, 20))


class _Op:
    __slots__ = ("eng", "fn", "deps", "signal", "sig", "kind", "qidx")


class Sched:
    ENGS = ("pe", "act", "dve", "pool", "sp")

    def __init__(self, nc):
        self.nc = nc
        self.ops = []
        self.lastw = {}
        self.rd_c = {}
        self.rd_d = {}
        self.pending = {}
        self.last_op = {e: None for e in self.ENGS}
        self.dmas = []
        self.qcount = {"pool": 0, "sp": 0}
        self.ncc = 0

    def add(self, eng, fn, reads=(), writes=(), kind="c", nobarrier=False):
        op = _Op()
        op.eng, op.fn, op.kind, op.deps, op.signal, op.sig, op.qidx = eng, fn, kind, set(), False, None, None
        for r in reads:
            w = self.lastw.get(r)
            if w is not None:
                op.deps.add(w)
        for k in writes:
            w = self.lastw.get(k)
            if w is not None:
                op.deps.add(w)
            for o in self.rd_c.get(k, {}).values():
                op.deps.add(o)
            for o in self.rd_d.get(k, ()):
                op.deps.add(o)
        if not nobarrier and eng in self.pending:
            op.deps |= self.pending.pop(eng)
        for r in reads:
            if kind == "c":
                self.rd_c.setdefault(r, {})[eng] = op
            else:
                self.rd_d.setdefault(r, []).append(op)
        for k in writes:
            self.lastw[k] = op
            self.rd_c[k] = {}
            self.rd_d[k] = []
        if kind == "dma":
            op.qidx = self.qcount[eng]
            self.qcount[eng] += 1
        elif kind == "cc":
            op.qidx = self.ncc
            self.ncc += 1
        if kind != "c":
            self.dmas.append(op)
        self.ops.append(op)
        self.last_op[eng] = op
        return op

    def barrier(self):
        deps = set(o for o in self.last_op.values() if o is not None) | set(self.dmas)
        for e in self.ENGS:
            self.pending[e] = set(deps) | self.pending.get(e, set())
        self.dmas = []

    def emit(self):
        nc = self.nc
        for op in self.ops:
            for d in op.deps:
                if d.kind == "c" and d.eng == "pe" and op.eng == "pe" and op.kind == "c":
                    continue
                d.signal = True
        import contextlib
        with contextlib.ExitStack() as st:
            csem = {e: st.enter_context(nc.semaphore("c_" + e)) for e in ("pe", "act", "dve")}
            qsem = {q: [st.enter_context(nc.semaphore("q_%s%d" % (q, i))) for i in range(NS_DMA)]
                    for q in ("pool", "sp")}
            ccsem = [st.enter_context(nc.semaphore("cc%d" % i)) for i in range(max(1, self.ncc))]
            cnt = {e: 0 for e in csem}
            byq = {"pool": [], "sp": []}
            for op in self.ops:
                if op.kind == "c":
                    if op.signal:
                        cnt[op.eng] += 1
                        op.sig = (csem[op.eng], cnt[op.eng], 1)
                elif op.kind == "dma":
                    i = op.qidx
                    op.sig = (qsem[op.eng][i % NS_DMA], 16 * (i // NS_DMA + 1), 16)
                    byq[op.eng].append(op)
                else:
                    op.sig = (ccsem[op.qidx], 1, None)
            block = st.enter_context(nc.Block())
            ops = self.ops

            def run(engname):
                def body(e):
                    known = {}
                    for op in ops:
                        if op.eng != engname:
                            continue
                        need = {}
                        for d in op.deps:
                            if d.sig is None:
                                continue
                            if d.kind == "c" and d.eng == "pe" and engname == "pe" and op.kind == "c":
                                continue
                            s, v, _ = d.sig
                            if need.get(id(s), (None, 0))[1] < v:
                                need[id(s)] = (s, v)
                        if op.kind == "dma" and op.qidx >= NS_DMA:
                            p = byq[engname][op.qidx - NS_DMA]
                            s, v, _ = p.sig
                            if need.get(id(s), (None, 0))[1] < v:
                                need[id(s)] = (s, v)
                        for sid, (s, v) in need.items():
                            if known.get(sid, 0) >= v:
                                continue
                            e.wait_ge(s, v)
                            known[sid] = v
                        ins = op.fn(e)
                        if op.kind == "c":
                            if op.signal:
                                ins.then_inc(op.sig[0], 1)
                        elif op.kind == "dma":
                            ins.then_inc(op.sig[0], 16)
                        else:
                            ins.then_inc(op.sig[0])
                    if engname in byq:
                        last = {}
                        for op in byq[engname]:
                            last[id(op.sig[0])] = (op.sig[0], op.sig[1])
                        for sid, (s, v) in last.items():
                            if known.get(sid, 0) < v:
                                e.wait_ge(s, v)
                return body

            block.tensor(run("pe"))
            block.scalar(run("act"))
            block.vector(run("dve"))
            block.gpsimd(run("pool"))
            block.sync(run("sp"))


def _piece(colf, bk, pn):
    ap = colf(bk)
    if ap.shape[-1] != pn:
        ap = ap[:, 0:pn]
    return ap


class Rot:
    def __init__(self, items):
        self.items = list(items)
        self.i = 0

    def next(self):
        v = self.items[self.i % len(self.items)]
        self.i += 1
        return v


def _t5_bucket(dist):
    max_exact = 16
    safe = np.maximum(dist, 1)
    large = max_exact + (np.log(safe / max_exact) / np.log(2048 / max_exact) * (32 - max_exact)).astype(np.int64)
    large = np.minimum(large, 31)
    return np.where(dist < max_exact, dist, large).astype(np.int64)


def _bias_index_and_mask(half):
    j = np.arange(128)[:, None]
    i = np.arange(128)[None, :]
    idx = np.zeros((128, 832), np.int64)
    msk = np.zeros((128, 832), np.float32)
    ph = 0.0 if half == 1 else NEG
    cur_rel = i - j
    prev_rel = 128 + i - j
    cur_ok = (j <= i)
    prev_ok = (i <= j)
    for off, dil in ((0, 1), (256, 4)):
        idx[:, off:off + 128] = _t5_bucket(np.clip(cur_rel, 0, None) * dil)
        msk[:, off:off + 128] = np.where(cur_ok, 0.0, NEG)
        idx[:, off + 128:off + 256] = _t5_bucket(np.clip(prev_rel, 0, None) * dil)
        msk[:, off + 128:off + 256] = np.where(prev_ok, 0.0, NEG)
    i3 = np.arange(64)[None, :]
    rel3 = 64 + i3 - j
    idx[:, 512:576] = _t5_bucket(np.clip(rel3, 0, None) * 16)
    m3 = np.where(rel3 >= 0, 0.0, NEG).astype(np.float32)
    m3[:64, :] += ph
    msk[:, 512:576] = m3
    for off, dil in ((576, 1), (704, 4)):
        idx[:, off:off + 128] = _t5_bucket(np.clip(prev_rel, 0, None) * dil)
        msk[:, off:off + 128] = np.where(prev_ok, 0.0, NEG) + ph
    return idx, msk


def _consts():
    c = np.zeros((128, 512), np.float32)
    j = np.arange(128)[:, None]
    i = np.arange(128)[None, :]
    c[:, 0:128] = np.eye(128, dtype=np.float32)
    c[:, 128:256] = np.where(j <= i, -1.0 / 16.0, 0.0)
    c[:, 256:384] = np.where(j <= i, 1.0, 0.0)
    c[:, 384:512] = 1.0
    return c


class _Stop(Exception):
    pass


def build(layers=(0, 1), dbg=(), stop=None):
    nc = bass.Bass("TRN2", target_bir_lowering=False)
    dt = lambda name, shape, dtype=F32: nc.dram_tensor(name, shape, dtype, kind="ExternalInput").ap()
    xT_d = dt("xT", [D, T])
    NL = len(layers)
    upto = 99 if stop is None else stop
    w_in_d = dt("w_in", [NL, D, N_IN])
    w_out_d = dt("w_out", [NL, D, D]) if upto > 4 else None
    w_gate_d = dt("w_gate", [NL, D, FFN]) if upto > 5 else None
    w_up_d = dt("w_up", [NL, D, FFN]) if upto > 5 else None
    w_down_d = dt("w_down", [NL, FFN, D]) if upto > 5 else None
    gpar_d = dt("gpar", [128, 80])
    w2_d = dt("w2", [16, 2 * 512])
    gb_d = dt("gb", [1, 2 * 512])
    cst_d = dt("cst", [128, 512])
    bias_d = dt("biasT", [8, 128, 832])
    mask_d = dt("maskT", [128, 832])
    yT_d = nc.dram_tensor("yT", [D, T], F32, kind="ExternalOutput").ap()
    dbg_d = {name: nc.dram_tensor("dbg_" + name, list(shape), (BF16 if dtn == "bf16" else F32), kind="ExternalOutput").ap()
             for name, shape, dtn in dbg}

    kloc = [nc.dram_tensor("kloc%d" % l, [1024, 1024], BF16) for l in range(2)]
    vloc = [nc.dram_tensor("vloc%d" % l, [1024, 1024], BF16) for l in range(2)]
    kall = [nc.dram_tensor("kall%d" % l, [2048, 1024], BF16) for l in range(2)]
    vall = [nc.dram_tensor("vall%d" % l, [2048, 1024], BF16) for l in range(2)]
    sfl = [nc.dram_tensor("sfl%d" % l, [512, 256], F32) for l in range(2)]
    sfa = [nc.dram_tensor("sfa%d" % l, [1024, 256], F32) for l in range(2)]
    park_o = [nc.dram_tensor("parko%d" % l, [4, 128, 2048], F32) for l in range(2)]
    park_q = [nc.dram_tensor("parkq%d" % l, [4, 128, 1024], BF16) for l in range(2)]

    import contextlib
    with contextlib.ExitStack() as st:
        sb = lambda name, shape, dtype: st.enter_context(nc.sbuf_tensor(name, shape, dtype))
        xT = sb("xT_sb", [128, NCH, T], F32)
        hT = sb("hT_sb", [128, NCH, T], BF16)
        BIG = sb("big", [128, 16384], F32)
        wsl = [sb("wsl%d" % i, [128, 4096], BF16) for i in range(NSLOT)]
        cst = sb("cst_sb", [128, 512], F32)
        gpar = sb("gpar_sb", [128, 80], F32)
        gsc = sb("gsc_sb", [128, 80], F32)
        cbf = sb("cbf_sb", [128, 256], BF16)
        wga = sb("wga_sb", [128, 2 * NCH * 16], BF16)
        msk = sb("msk_sb", [128, 832], F32)
        ps = [st.enter_context(nc.psum_tensor("ps%d" % i, [128, 1024], F32)) for i in range(4)]

        S = Sched(nc)
        ident_f = cst[:, 0:128]
        tri_f = cst[:, 128:256]
        caus_f = cst[:, 256:384]
        ones_f = cst[:, 384:512]
        ident_b = cbf[:, 0:128]
        ones_b = cbf[:, 128:256]

        def bank(b):
            return ps[b // 2][:, (b % 2) * 512:(b % 2 + 1) * 512]

        def pk(*banks):
            return [("ps", b) for b in banks]

        def bf(off, n):
            return BIG[:, off:off + n // 2].bitcast(BF16)

        def ff(off, n):
            return BIG[:, off:off + n]

        S.add("sp", lambda e: e.dma_start(out=cst[:], in_=cst_d), writes=["cst"], kind="dma")
        S.add("sp", lambda e: e.dma_start(out=gpar[:], in_=gpar_d), writes=["gpar"], kind="dma")
        S.add("sp", lambda e: e.dma_start(out=msk[:], in_=mask_d), writes=["msk"], kind="dma")
        for q in range(4):
            S.add("sp", lambda e, q=q: e.dma_start(
                out=xT[:, 4 * q:4 * q + 4, :],
                in_=xT_d[512 * q:512 * (q + 1), :].rearrange("(c p) t -> p c t", p=128)),
                writes=[("x", c, t) for c in range(4 * q, 4 * q + 4) for t in range(2)], kind="dma")
        for li_, l in enumerate(layers):
            S.add("pool", lambda e, l=l, li_=li_: e.dma_start(
                out=wga[:, l * 256:(l + 1) * 256].rearrange("p (c n) -> p c n", n=16),
                in_=w_in_d[li_, :, C_GA:C_GA + 16].rearrange("(c p) n -> p c n", p=128)),
                writes=[("wga", l)], kind="dma", nobarrier=True)
        S.add("dve", lambda e: e.tensor_copy(out=cbf[:, 0:128], in_=cst[:, 0:128]), reads=["cst"], writes=["cbf0"])
        S.add("dve", lambda e: e.tensor_copy(out=cbf[:, 128:256], in_=cst[:, 384:512]), reads=["cst"], writes=["cbf1"])
        S.add("dve", lambda e: e.tensor_scalar(out=gsc[:, 0:64], in0=gpar[:, 0:64], scalar1=float(np.sqrt(D)),
                                               scalar2=None, op0=ALU.mult), reads=["gpar"], writes=["gsc0"])
        S.add("dve", lambda e: e.tensor_scalar(out=gsc[:, 64:68], in0=gpar[:, 64:68], scalar1=16.0,
                                               scalar2=None, op0=ALU.mult), reads=["gpar"], writes=["gsc1"])
        S.add("dve", lambda e: e.tensor_scalar(out=gsc[:, 68:70], in0=gpar[:, 68:70], scalar1=1.0,
                                               scalar2=None, op0=ALU.mult), reads=["gpar"], writes=["gsc2"])
        S.add("dve", lambda e: e.tensor_scalar(out=gsc[:, 70:72], in0=gpar[:, 70:72], scalar1=float(np.sqrt(128.0)),
                                               scalar2=None, op0=ALU.mult), reads=["gpar"], writes=["gsc3"])
        GS = ["gsc0", "gsc1", "gsc2", "gsc3", "gpar"]
        CB = ["cbf0", "cbf1", "cst"]
        flag = gpar[:, 72:73]

        wseq = []

        def wreq(src, a, b):
            wseq.append((src, a, b))
            return len(wseq) - 1

        wstate = {"issued": 0}

        def wget(i):
            while wstate["issued"] < min(len(wseq), i + NSLOT - 1):
                k = wstate["issued"]
                src, a, b = wseq[k]
                slot = k % NSLOT
                dst = wsl[slot][:, 0:a * b].rearrange("p (a b) -> p a b", b=b)
                S.add("pool", lambda e, dst=dst, src=src: e.dma_start(out=dst, in_=src),
                      writes=[("w", slot)], kind="dma", nobarrier=True)
                wstate["issued"] += 1
            src, a, b = wseq[i]
            slot = i % NSLOT
            return wsl[slot][:, 0:a * b].rearrange("p (a b) -> p a b", b=b), ("w", slot)

        def wcols(wd, l, c0, n):
            return wd[layers.index(l), :, c0:c0 + n].rearrange("(c p) n -> p c n", p=128)

        plan = {}
        for li_, l in enumerate(layers):
            p = {}
            plan[l] = p
            p["kb"] = [wreq(wcols(w_in_d, l, C_KB + 256 * i, 256), 16, 256) for i in range(4)]
            p["vb"] = [wreq(wcols(w_in_d, l, C_VB + 256 * i, 256), 16, 256) for i in range(4)]
            p["qa"] = [wreq(wcols(w_in_d, l, C_QA + 256 * i, 256), 16, 256) for i in range(2)]
            p["ka"] = [wreq(wcols(w_in_d, l, C_KA + 256 * i, 256), 16, 256) for i in range(2)]
            p["va"] = [wreq(wcols(w_in_d, l, C_VA + 256 * i, 256), 16, 256) for i in range(4)]
            p["qb"] = [wreq(wcols(w_in_d, l, C_QB + 256 * i, 256), 16, 256) for i in range(4)]
            p["ra"] = [wreq(wcols(w_in_d, l, C_RA + 256 * i, 256), 16, 256) for i in range(4)]
            if upto <= 4:
                continue
            p["wo"] = [wreq(wcols(w_out_d, l, 256 * i, 256), 16, 256) for i in range(8)]
            p["ffn"] = []
            if upto <= 5:
                continue
            for (j0, nj) in FFN_PARTS:
                gu = []
                for s in range(nj // 2):
                    c0 = (j0 + 2 * s) * 128
                    gu.append((wreq(wcols(w_gate_d, l, c0, 256), 16, 256), wreq(wcols(w_up_d, l, c0, 256), 16, 256)))
                dn = [wreq(w_down_d[li_, j0 * 128:(j0 + nj) * 128, n * 128:(n + 1) * 128]
                           .rearrange("(j p) n -> p j n", p=128), nj, 128) for n in range(16)]
                p["ffn"].append((gu, dn))

        prot = Rot([0, 1])
        srot = Rot([4, 5, 6, 7])

        def dump(name, ap, reads):
            if name in dbg_d:
                S.add("sp", lambda e: e.dma_start(out=dbg_d[name], in_=ap), reads=reads, kind="dma")

        def rmsnorm(gcol0, tag):
            sq_off = 0
            for t in range(2):
                b = srot.next()
                for c in range(NCH):
                    sq = bf(sq_off + (c % 2) * 256, 512)
                    S.add("act", lambda e, sq=sq, c=c, t=t: e.activation(out=sq, in_=xT[:, c, t * 512:(t + 1) * 512], func=AF.Square),
                          reads=[("x", c, t)], writes=[("nsq", c % 2)])
                    S.add("pe", lambda e, sq=sq, c=c, b=b: e.matmul(bank(b), lhsT=ones_b, rhs=sq, start=(c == 0), stop=(c == NCH - 1)),
                          reads=[("nsq", c % 2)] + CB, writes=pk(b))
                rstd = ff(sq_off + 512 + t * 512, 512)
                S.add("act", lambda e, rstd=rstd, b=b: e.activation(out=rstd, in_=bank(b), func=AF.Sqrt, bias=float(D * EPS)),
                      reads=pk(b), writes=[("nrstd", t)])
                S.add("dve", lambda e, rstd=rstd: e.reciprocal(out=rstd, in_=rstd), reads=[("nrstd", t)], writes=[("nrstd", t)])
                for c in range(NCH):
                    S.add("dve", lambda e, rstd=rstd, c=c, t=t: e.scalar_tensor_tensor(
                        out=hT[:, c, t * 512:(t + 1) * 512], in0=xT[:, c, t * 512:(t + 1) * 512],
                        scalar=gsc[:, gcol0 + c:gcol0 + c + 1], in1=rstd, op0=ALU.mult, op1=ALU.mult),
                        reads=[("x", c, t), ("nrstd", t)] + GS, writes=[("h", c, t)])

        def norm_stats_chunk(c, scr_off, extra):
            for t in range(2):
                i = (2 * c + t) % 2
                sq = bf(scr_off + i * 256, 512)
                S.add("act", lambda e, sq=sq, c=c, t=t: e.activation(out=sq, in_=xT[:, c, t * 512:(t + 1) * 512], func=AF.Square),
                      reads=[("x", c, t)], writes=[("nsq", i)] + extra)
                S.add("pe", lambda e, sq=sq, c=c, t=t: e.matmul(bank(6 + t), lhsT=ones_b, rhs=sq, start=(c == 0), stop=(c == NCH - 1)),
                      reads=[("nsq", i)] + CB, writes=pk(6 + t))

        def norm_finish(gcol0):
            for t in range(2):
                rstd = ff(512 + t * 512, 512)
                S.add("act", lambda e, rstd=rstd, t=t: e.activation(out=rstd, in_=bank(6 + t), func=AF.Sqrt, bias=float(D * EPS)),
                      reads=pk(6 + t), writes=[("nrstd", t)])
                S.add("dve", lambda e, rstd=rstd: e.reciprocal(out=rstd, in_=rstd), reads=[("nrstd", t)], writes=[("nrstd", t)])
                for c in range(NCH):
                    S.add("dve", lambda e, rstd=rstd, c=c, t=t: e.scalar_tensor_tensor(
                        out=hT[:, c, t * 512:(t + 1) * 512], in0=xT[:, c, t * 512:(t + 1) * 512],
                        scalar=gsc[:, gcol0 + c:gcol0 + c + 1], in1=rstd, op0=ALU.mult, op1=ALU.mult),
                        reads=[("x", c, t), ("nrstd", t)] + GS, writes=[("h", c, t)])

        HALL = [[("h", c, t) for c in range(NCH)] for t in range(2)]

        def proj_fm(wv, wkey, col0, pair):
            for t in range(2):
                def fn(e, t=t):
                    ins = None
                    for k in range(NCH):
                        ins = e.matmul(ps[pair][:, t * 512:(t + 1) * 512], lhsT=wv[:, k, col0:col0 + 128],
                                       rhs=hT[:, k, t * 512:(t + 1) * 512], start=(k == 0), stop=(k == NCH - 1))
                    return ins
                S.add("pe", fn, reads=[wkey] + HALL[t], writes=pk(2 * pair + t))

        def headnorm_fm(pair, nparts_eps, gcol, out_bf, okey, sq_off, sbanks=None):
            sq = bf(sq_off, 1024)
            S.add("act", lambda e: e.activation(out=sq, in_=ps[pair][:, :], func=AF.Square),
                  reads=pk(2 * pair, 2 * pair + 1), writes=["hn_sq"])
            b0, b1 = sbanks if sbanks is not None else (srot.next(), srot.next())
            for t, b in ((0, b0), (1, b1)):
                S.add("pe", lambda e, t=t, b=b: e.matmul(bank(b), lhsT=ones_b, rhs=sq[:, t * 512:(t + 1) * 512], start=True, stop=True),
                      reads=["hn_sq"] + CB, writes=pk(b))
            rstd = ff(sq_off + 512, 1024)
            for t, b in ((0, b0), (1, b1)):
                S.add("act", lambda e, t=t, b=b: e.activation(out=rstd[:, t * 512:(t + 1) * 512], in_=bank(b), func=AF.Sqrt, bias=float(nparts_eps)),
                      reads=pk(b), writes=[("hn_rstd", t)])
                S.add("dve", lambda e, t=t: e.reciprocal(out=rstd[:, t * 512:(t + 1) * 512], in_=rstd[:, t * 512:(t + 1) * 512]),
                      reads=[("hn_rstd", t)], writes=[("hn_rstd", t)])
            S.add("dve", lambda e: e.scalar_tensor_tensor(out=out_bf, in0=ps[pair][:, :], scalar=gsc[:, gcol:gcol + 1], in1=rstd,
                                                          op0=ALU.mult, op1=ALU.mult),
                  reads=pk(2 * pair, 2 * pair + 1) + [("hn_rstd", 0), ("hn_rstd", 1)] + GS, writes=[okey])

        def mix(k):
            off = 4096 + k * 512 if k < 8 else (k - 8) * 512
            return bf(off, 1024)

        prot4 = Rot([0, 1, 2, 3])
        out_done = [False]
        srot8 = Rot([0, 1, 2, 3, 4, 5, 6, 7])
        srot6 = Rot([2, 3, 4, 5, 6, 7])

        try:
            def do_layer(li, l, pre_normed, fuse_next):
                P = plan[l]
                KL, VL, KA, VA = kloc[l], vloc[l], kall[l], vall[l]
                if pre_normed:
                    norm_finish(l * 16)
                else:
                    rmsnorm(l * 16, "n1")
                if li == 0:
                    dump("h", hT[:, 3, :], HALL[0] + HALL[1])
                if stop == 0:
                    raise _Stop()
                S.barrier()
                kst = [bf(2048 + i * 512, 1024) for i in range(2)]
                vst = [bf(3072 + i * 128, 256) for i in range(4)]
                kvkeys = []
                for s4 in range(4):
                    wv, wkey = wget(P["kb"][s4])
                    for hh in range(2):
                        h = 2 * s4 + hh
                        pair = prot.next()
                        proj_fm(wv, wkey, hh * 128, pair)
                        headnorm_fm(pair, 128 * EPS, 70 + l, kst[h % 2], ("kst", h % 2), 0)
                        S.add("sp", lambda e, h=h: e.dma_start(out=KL.ap()[h * 128:(h + 1) * 128, :], in_=kst[h % 2]),
                              reads=[("kst", h % 2)], writes=[("kvloc", l, "k", h)], kind="dma")
                        kvkeys.append(("kvloc", l, "k", h))
                if stop == 0.3:
                    raise _Stop()
                vi = 0
                for s4 in range(4):
                    wv, wkey = wget(P["vb"][s4])
                    for tb in range(8):
                        b = srot.next()

                        def fn(e, tb=tb, b=b, wv=wv):
                            ins = None
                            for k in range(NCH):
                                ins = e.matmul(bank(b)[:, 0:256], lhsT=hT[:, k, tb * 128:(tb + 1) * 128], rhs=wv[:, k, :],
                                               start=(k == 0), stop=(k == NCH - 1))
                            return ins
                        S.add("pe", fn, reads=[wkey] + HALL[tb // 4], writes=pk(b))
                        vs = vst[vi % 4]
                        S.add("act", lambda e, vs=vs, b=b: e.copy(out=vs, in_=bank(b)[:, 0:256]), reads=pk(b), writes=[("vst", vi % 4)])
                        S.add("sp", lambda e, vs=vs, tb=tb, s4=s4: e.dma_start(
                            out=VL.ap()[tb * 128:(tb + 1) * 128, s4 * 256:(s4 + 1) * 256], in_=vs),
                            reads=[("vst", vi % 4)], writes=[("kvloc", l, "v", s4, tb)], kind="dma")
                        kvkeys.append(("kvloc", l, "v", s4, tb))
                        vi += 1
                if stop == 0.6:
                    raise _Stop()
                S.add("pool", lambda e: e.collective_compute("AllGather", ALU.bypass, replica_groups=[[0, 1], [2, 3], [4, 5], [6, 7]],
                                                             ins=[KL.ap().opt()], outs=[KA.ap().opt()]),
                      reads=kvkeys, writes=[("kall", l)], kind="cc", nobarrier=True)
                S.add("pool", lambda e: e.collective_compute("AllGather", ALU.bypass, replica_groups=[[0, 1], [2, 3], [4, 5], [6, 7]],
                                                             ins=[VL.ap().opt()], outs=[VA.ap().opt()]),
                      reads=kvkeys, writes=[("vall", l)], kind="cc", nobarrier=True)
                if stop == 1:
                    raise _Stop()
                S.barrier()

                qT = [bf(512 * h, 1024) for h in range(4)]
                kT = [bf(2048 + 512 * h, 1024) for h in range(4)]
                vtok = bf(4096, 8192).rearrange("p (a b) -> p a b", b=1024)
                lp = ff(8192, 4096).rearrange("p (a b) -> p a b", b=512)
                g1T = BIG[0:16, 12288:13312]
                E = ff(13312, 1024)
                qcst = [bf(14336 + 512 * i, 1024) for i in range(2)]
                dect = ff(15360, 32)
                cdt = ff(15392, 32)
                w2l = BIG[0:32, 15424:15936]
                g1T32 = BIG[0:32, 12288:13312]
                S.add("dve", lambda e: e.memset(w2l, 0.0), writes=["w2", "gb"])
                S.add("dve", lambda e: e.memset(g1T32, 1.0), writes=["g1T"])
                S.add("sp", lambda e: e.dma_start(out=BIG[0:16, 15424:15936], in_=w2_d[:, l * 512:(l + 1) * 512]), writes=["w2"], kind="dma")
                S.add("sp", lambda e: e.dma_start(out=BIG[16:17, 15424:15936], in_=gb_d[:, l * 512:(l + 1) * 512]), writes=["gb"], kind="dma")
                for name, plist, dst in (("qa", P["qa"], qT), ("ka", P["ka"], kT)):
                    for s2 in range(2):
                        wv, wkey = wget(plist[s2])
                        for hh in range(2):
                            h = 2 * s2 + hh
                            pair = prot.next()
                            proj_fm(wv, wkey, hh * 128, pair)
                            S.add("act", lambda e, pair=pair, d=dst[h]: e.copy(out=d, in_=ps[pair][:, :]),
                                  reads=pk(2 * pair, 2 * pair + 1), writes=[(name, h)])
                if stop == 1.2:
                    raise _Stop()
                for s4 in range(4):
                    wv, wkey = wget(P["va"][s4])
                    for tb in range(8):
                        b = srot.next()

                        def fn(e, tb=tb, b=b, wv=wv):
                            ins = None
                            for k in range(NCH):
                                ins = e.matmul(bank(b)[:, 0:256], lhsT=hT[:, k, tb * 128:(tb + 1) * 128], rhs=wv[:, k, :],
                                               start=(k == 0), stop=(k == NCH - 1))
                            return ins
                        S.add("pe", fn, reads=[wkey] + HALL[tb // 4], writes=pk(b))
                        S.add("act", lambda e, tb=tb, b=b, s4=s4: e.copy(out=vtok[:, tb, s4 * 256:(s4 + 1) * 256], in_=bank(b)[:, 0:256]),
                              reads=pk(b), writes=[("va", tb, s4)])
                if stop == 1.4:
                    raise _Stop()
                pair = prot.next()
                wgl = wga[:, l * 256:(l + 1) * 256].rearrange("p (c n) -> p c n", n=16)
                for t in range(2):
                    def fn(e, t=t, pair=pair, wgl=wgl):
                        ins = None
                        for k in range(NCH):
                            ins = e.matmul(ps[pair][0:16, t * 512:(t + 1) * 512], lhsT=wgl[:, k, :], rhs=hT[:, k, t * 512:(t + 1) * 512],
                                           start=(k == 0), stop=(k == NCH - 1))
                        return ins
                    S.add("pe", fn, reads=[("wga", l)] + HALL[t], writes=pk(2 * pair + t))
                S.add("act", lambda e, pair=pair: e.copy(out=g1T, in_=ps[pair][0:16, :]), reads=pk(2 * pair, 2 * pair + 1), writes=["g1T"])
                if stop == 1.5:
                    raise _Stop()
                for tb in range(8):
                    b = srot.next()

                    def fn(e, tb=tb, b=b):
                        return e.matmul(bank(b), lhsT=g1T32[:, tb * 128:(tb + 1) * 128], rhs=w2l, start=True, stop=True)
                    S.add("pe", fn, reads=["g1T", "w2", "gb", "cst"], writes=pk(b))
                    S.add("act", lambda e, tb=tb, b=b: e.activation(out=lp[:, tb, :], in_=bank(b), func=AF.Exp, scale=-1.0),
                          reads=pk(b), writes=[("lp", tb)])
                    S.add("act", lambda e, tb=tb: e.activation(out=lp[:, tb, :], in_=lp[:, tb, :], func=AF.Ln, bias=1.0),
                          reads=[("lp", tb)], writes=[("lp", tb)])
                if stop == 1.6:
                    raise _Stop()
                if li == 0:
                    dump("lp", lp[:, 0, :], [("lp", 0)])
                for h in range(4):
                    pair = prot.next()
                    for n in range(8):
                        S.add("pe", lambda e, n=n, pair=pair, h=h: e.matmul(ps[pair][:, n * 128:(n + 1) * 128], lhsT=lp[:, n, h * 128:(h + 1) * 128],
                                                                            rhs=tri_f, start=True, stop=True),
                              reads=[("lp", n), "cst"], writes=pk(2 * pair + n // 4))
                    PP = pk(2 * pair, 2 * pair + 1)
                    S.add("act", lambda e, pair=pair: e.activation(out=E, in_=ps[pair][:, :], func=AF.Exp), reads=PP, writes=["E"])
                    S.add("dve", lambda e, h=h: e.tensor_copy(out=dect[:, h * 8:(h + 1) * 8], in_=E.rearrange("p (a b) -> p a b", b=128)[:, :, 127]),
                          reads=["E"], writes=[("dec", h)])
                    S.add("dve", lambda e, h=h: e.memset(cdt[:, h * 8:h * 8 + 1], 1.0), writes=[("cd", h)])
                    for n in range(1, 8):
                        S.add("dve", lambda e, n=n, h=h: e.tensor_tensor(out=cdt[:, h * 8 + n:h * 8 + n + 1], in0=cdt[:, h * 8 + n - 1:h * 8 + n],
                                                                        in1=dect[:, h * 8 + n - 1:h * 8 + n], op=ALU.mult),
                              reads=[("cd", h), ("dec", h)], writes=[("cd", h)])
                    S.add("dve", lambda e, h=h: e.scalar_tensor_tensor(out=qT[h], in0=qT[h], scalar=float(128.0 ** -0.5), in1=E,
                                                                       op0=ALU.mult, op1=ALU.mult),
                          reads=[("qa", h), "E"], writes=[("qa", h)])
                    qc = qcst[h % 2]
                    S.add("dve", lambda e, h=h, qc=qc: e.tensor_tensor(
                        out=qc.rearrange("p (a b) -> p a b", b=128), in0=qT[h].rearrange("p (a b) -> p a b", b=128),
                        in1=cdt[:, h * 8:(h + 1) * 8].unsqueeze(2).to_broadcast([128, 8, 128]), op=ALU.mult),
                        reads=[("cd", h), ("qa", h)], writes=[("qcst", h % 2)])
                    S.add("sp", lambda e, h=h, qc=qc: e.dma_start(out=park_q[l].ap()[h], in_=qc), reads=[("qcst", h % 2)],
                          writes=[("parkq", l, h)], kind="dma")
                    S.add("act", lambda e, pair=pair: e.activation(out=E, in_=ps[pair][:, :], func=AF.Exp, scale=-1.0), reads=PP, writes=["E"])
                    S.add("dve", lambda e, h=h: e.tensor_tensor(out=kT[h], in0=kT[h], in1=E, op=ALU.mult),
                          reads=[("ka", h), "E"], writes=[("ka", h)])
                if li == 0:
                    dump("qt", qT[1], [("qa", 1)])
                    dump("kt", kT[1], [("ka", 1)])
                if stop == 2:
                    raise _Stop()
                S.barrier()
                ktok = [bf(8192 + 512 * h, 1024).rearrange("p (a b) -> p a b", b=128) for h in range(4)]
                S_f = [[ff(10240 + 512 * h + 256 * i, 256) for i in range(2)] for h in range(4)]
                S_b = [[bf(12288 + 256 * h + 128 * i, 256) for i in range(2)] for h in range(4)]
                am = [bf(13312 + 64 * i, 128) for i in range(4)]
                ost = [ff(13568 + 256 * i, 256) for i in range(4)]
                sfkeys = []
                for h in range(4):
                    kt_ = ktok[h]
                    b = srot8.next()
                    pbv = bank(b).bitcast(BF16)
                    for n in range(8):
                        S.add("pe", lambda e, n=n, pbv=pbv, h=h: e.transpose(pbv[:, n * 128:(n + 1) * 128], kT[h][:, n * 128:(n + 1) * 128], ident_b),
                              reads=[("ka", h)] + CB, writes=pk(b))
                    S.add("act", lambda e, pbv=pbv, kt_=kt_: e.copy(out=kt_, in_=pbv.rearrange("p (a b) -> p a b", b=128)),
                          reads=pk(b), writes=[("ktok", h)])
                ci = 0
                for n in range(8):
                    for h in range(4):
                        kt_ = ktok[h]
                        cs_ = slice(n * 128, (n + 1) * 128)
                        ba, bo, bc = srot8.next(), srot8.next(), srot8.next()
                        a_ = am[ci % 4]
                        o_ = ost[ci % 4]
                        Sn, So = S_f[h][n % 2], S_f[h][(n + 1) % 2]
                        Sbn, Sbo = S_b[h][n % 2], S_b[h][(n + 1) % 2]
                        kSn, kSo = ("Sf", h, n % 2), ("Sf", h, (n + 1) % 2)
                        kBn, kBo = ("Sb", h, n % 2), ("Sb", h, (n + 1) % 2)
                        S.add("pe", lambda e, ba=ba, h=h, cs_=cs_: e.matmul(bank(ba)[:, 0:128], lhsT=kT[h][:, cs_], rhs=qT[h][:, cs_], start=True, stop=True),
                              reads=[("ka", h), ("qa", h)], writes=pk(ba))
                        S.add("dve", lambda e, ba=ba, a_=a_: e.tensor_tensor(out=a_, in0=bank(ba)[:, 0:128], in1=caus_f, op=ALU.mult),
                              reads=pk(ba) + ["cst"], writes=[("am", ci % 4)])

                        def fo(e, bo=bo, n=n, h=h, a_=a_, Sbo=Sbo, cs_=cs_):
                            ins = None
                            for vc in range(2):
                                ins = e.matmul(bank(bo)[:, vc * 128:(vc + 1) * 128], lhsT=vtok[:, n, h * 256 + vc * 128:h * 256 + (vc + 1) * 128],
                                               rhs=a_, start=True, stop=(n == 0))
                                if n > 0:
                                    ins = e.matmul(bank(bo)[:, vc * 128:(vc + 1) * 128], lhsT=Sbo[:, vc * 128:(vc + 1) * 128],
                                                   rhs=qT[h][:, cs_], start=False, stop=True)
                            return ins
                        S.add("pe", fo, reads=[("am", ci % 4), ("qa", h)] + [("va", n, s4) for s4 in range(4)] + ([kBo] if n > 0 else []),
                              writes=pk(bo))
                        S.add("act", lambda e, bo=bo, o_=o_: e.copy(out=o_, in_=bank(bo)[:, 0:256]), reads=pk(bo), writes=[("ost", ci % 4)])
                        S.add("sp", lambda e, o_=o_, h=h, cs_=cs_: e.dma_start(
                            out=park_o[l].ap()[h].rearrange("p (a t) -> p a t", a=2)[:, :, cs_], in_=o_.rearrange("p (a b) -> p a b", a=2)),
                            reads=[("ost", ci % 4)], writes=[("parko", l, h, n)], kind="dma")
                        S.add("pe", lambda e, bc=bc, kt_=kt_, n=n, h=h: e.matmul(bank(bc)[:, 0:256], lhsT=kt_[:, n, :], rhs=vtok[:, n, h * 256:(h + 1) * 256],
                                                                               start=True, stop=True),
                              reads=[("ktok", h)] + [("va", n, s4) for s4 in range(4)], writes=pk(bc))
                        dcol = dect[:, h * 8 + n:h * 8 + n + 1]
                        if n == 0:
                            S.add("dve", lambda e, bc=bc, Sn=Sn, dcol=dcol: e.tensor_scalar(out=Sn, in0=bank(bc)[:, 0:256], scalar1=dcol, scalar2=None, op0=ALU.mult),
                                  reads=pk(bc) + [("dec", h)], writes=[kSn])
                        else:
                            S.add("dve", lambda e, bc=bc, Sn=Sn, So=So: e.tensor_tensor(out=Sn, in0=bank(bc)[:, 0:256], in1=So, op=ALU.add),
                                  reads=pk(bc) + [kSo], writes=[kSn])
                            S.add("dve", lambda e, Sn=Sn, dcol=dcol: e.tensor_scalar(out=Sn, in0=Sn, scalar1=dcol, scalar2=None, op0=ALU.mult),
                                  reads=[kSn, ("dec", h)], writes=[kSn])
                        if n < 7:
                            S.add("act", lambda e, Sn=Sn, Sbn=Sbn: e.copy(out=Sbn, in_=Sn), reads=[kSn], writes=[kBn])
                        else:
                            S.add("sp", lambda e, Sn=Sn, h=h: e.dma_start(out=sfl[l].ap()[h * 128:(h + 1) * 128, :], in_=Sn),
                                  reads=[kSn], writes=[("sfl", l, h)], kind="dma")
                            sfkeys.append(("sfl", l, h))
                        ci += 1
                S.add("pool", lambda e: e.collective_compute("AllGather", ALU.bypass, replica_groups=[[0, 1], [2, 3], [4, 5], [6, 7]],
                                                             ins=[sfl[l].ap().opt()], outs=[sfa[l].ap().opt()]),
                      reads=sfkeys, writes=[("sfa", l)], kind="cc", nobarrier=True)
                if stop == 3:
                    raise _Stop()
                S.barrier()

                qn = bf(5632, 1024)
                BMf = ff(9536, 832)
                ptb = [bf(10784 + 256 * i, 512) for i in range(3)]
                rsb = ff(11552, 1024)

                def hbufs(s_):
                    if s_ == 0:
                        o = dict(K=6144, V1=7168, V2=7744, V3=8512, BMb=10368)
                    else:
                        o = dict(K=12576, V1=13600, V2=14176, V3=14944, BMb=15968)
                    return dict(Kall=bf(o["K"], 2048),
                                V1=bf(o["V1"], 1152).rearrange("p (a b) -> p a b", b=128),
                                V2=bf(o["V2"], 1536).rearrange("p (a r b) -> p a r b", a=3, r=4),
                                V3=bf(o["V3"], 2048).rearrange("p (a b) -> p a b", b=128),
                                BMb=bf(o["BMb"], 832))
                HB = [hbufs(0), hbufs(1)]
                KAa, VAa, KLa, VLa = KA.ap(), VA.ap(), KL.ap(), VL.ap()

                def head_loads(h):
                    s_ = h % 2
                    B_ = HB[s_]
                    hc = slice(h * 128, (h + 1) * 128)
                    Kall, V1, V2, V3 = B_["Kall"], B_["V1"], B_["V2"], B_["V3"]
                    vkeys = [("kvloc", l, "v", h // 2, tb) for tb in range(8)]
                    ld = lambda fn, reads, key: S.add("sp", fn, reads=reads, writes=[(key, s_)], kind="dma")
                    ld(lambda e: e.dma_start(out=Kall[:, 0:1024], in_=KAa[h * 128:(h + 1) * 128, :]), [("kall", l)], "Kall0")
                    ld(lambda e: e.dma_start(out=Kall[:, 1024:2048], in_=KLa[h * 128:(h + 1) * 128, :]), [("kvloc", l, "k", h)], "Kall1")
                    ld(lambda e: e.dma_start(out=V1[:, 0, :], in_=VAa[896:1024, hc]), [("vall", l)], "V1p")
                    ld(lambda e: e.dma_start(out=V1[:, 1:9, :], in_=VLa[0:1024, hc].rearrange("(a p) d -> p a d", p=128)), vkeys, "V1l")
                    ld(lambda e: e.dma_start(out=V2[:, 0, :, :], in_=VAa[512:1024, hc].rearrange("(i r) d -> i r d", r=4)), [("vall", l)], "V2p")
                    ld(lambda e: e.dma_start(out=V2[:, 1, :, :], in_=VLa[0:512, hc].rearrange("(i r) d -> i r d", r=4)), vkeys, "V2a")
                    ld(lambda e: e.dma_start(out=V2[:, 2, :, :], in_=VLa[512:1024, hc].rearrange("(i r) d -> i r d", r=4)), vkeys, "V2b")
                    ld(lambda e: e.dma_start(out=V3[0:64, :, :], in_=VAa[0:1024, hc].rearrange("(j r) d -> j r d", r=16)), [("vall", l)], "V3p")
                    ld(lambda e: e.dma_start(out=V3[64:128, :, :], in_=VLa[0:1024, hc].rearrange("(j r) d -> j r d", r=16)), vkeys, "V3l")
                    S.add("sp", lambda e: e.dma_start(out=BMf, in_=bias_d[h]), writes=["BMf"], kind="dma")
                    S.add("dve", lambda e: e.tensor_tensor(out=B_["BMb"], in0=BMf, in1=msk[:, :], op=ALU.add), reads=["BMf", "msk"], writes=[("BMb", s_)])

                lrot = Rot([4, 5, 6, 7])
                pti = [0]
                head_loads(0)
                for h in range(8):
                    s_ = h % 2
                    B_ = HB[s_]
                    Kall, V1, V2, V3, BMb = B_["Kall"], B_["V1"], B_["V2"], B_["V3"], B_["BMb"]
                    if h % 2 == 0:
                        wv, wkey = wget(P["qb"][h // 2])
                    if h + 1 < 8:
                        head_loads(h + 1)
                    proj_fm(wv, wkey, (h % 2) * 128, 3)
                    headnorm_fm(3, 128 * EPS, 68 + l, qn, "qn", 4096, sbanks=(4, 5))
                    KK = [("Kall0", s_), ("Kall1", s_), ("BMb", s_), "qn"] + CB
                    first = {0: True, 1: True, 2: True, 3: True}
                    groups = []
                    items = []
                    for kbi in range(9):
                        keys = Kall[:, 896 + kbi * 128:896 + (kbi + 1) * 128]
                        if kbi == 0:
                            q0, q1, bm = 0, 128, BMb[:, 576:704]
                        elif kbi == 8:
                            q0, q1, bm = 896, 1024, BMb[:, 0:128]
                        else:
                            q0, q1, bm = (kbi - 1) * 128, (kbi + 1) * 128, BMb[:, 0:256]
                        pieces = []
                        a = q0
                        while a < q1:
                            b_ = min(q1, (a // 512 + 1) * 512)
                            pieces.append((a // 512, (lambda bk, a=a, b_=b_: bk[:, a % 512:(b_ - 1) % 512 + 1]), a - q0, b_ - a))
                            a = b_
                        items.append((keys, qn[:, q0:q1], bm, q1 - q0, None, V1[:, kbi, :], [("V1p", s_), ("V1l", s_)], pieces))
                    groups += [items[0:2], items[2:4], items[4:6], items[6:8], items[8:9]]
                    qn4 = qn.rearrange("p (n i r) -> p n i r", n=2, r=4)
                    K4 = Kall.rearrange("p (n i r) -> p n i r", n=4, r=4)
                    for r in range(4):
                        colf = lambda bk, r=r: bk.rearrange("p (i r) -> p i r", r=4)[:, :, r]
                        g = [(K4[:, 1, :, r], qn4[:, 0, :, r], BMb[:, 704:832], 128, None, V2[:, 0, r, :], [("V2p", s_)], [(0, colf, 0, 128)]),
                             (K4[:, 2, :, r], qn4[:, :, :, r], BMb[:, 256:512], 256, 128, V2[:, 1, r, :], [("V2a", s_)], [(0, colf, 0, 128), (1, colf, 128, 128)]),
                             (K4[:, 3, :, r], qn4[:, 1, :, r], BMb[:, 256:384], 128, None, V2[:, 2, r, :], [("V2b", s_)], [(1, colf, 0, 128)])]
                        groups.append(g)
                    qn16 = qn.rearrange("p (i r) -> p i r", r=16)
                    K16 = Kall.rearrange("p (j r) -> p j r", r=16)
                    for r0 in (0, 8):
                        g = []
                        for r in range(r0, r0 + 8):
                            colf = lambda bk, r=r: bk.rearrange("p (i r) -> p i r", r=16)[:, :, r]
                            g.append((K16[:, :, r], qn16[:, :, r], BMb[:, 512:576], 64, None, V3[:, r, :], [("V3p", s_), ("V3l", s_)],
                                      [(0, colf, 0, 32), (1, colf, 32, 32)]))
                        groups.append(g)
                    def rec_log(g):
                        bl = lrot.next()
                        W_ = sum(it[3] for it in g)
                        pt = ptb[pti[0] % 3]
                        ptk = ("ptb", pti[0] % 3)
                        pti[0] += 1

                        def f_log(e, g=g, bl=bl):
                            off = 0
                            ins = None
                            for (keys, q, bm, n_, sp_, v, vk, pieces) in g:
                                o_ = bank(bl)[:, off:off + n_]
                                if sp_ is not None:
                                    o_ = o_.rearrange("p (a b) -> p a b", b=sp_)
                                e.matmul(o_, lhsT=keys, rhs=q, start=True, stop=False)
                                ins = e.matmul(bank(bl)[:, off:off + n_], lhsT=ident_b, rhs=bm, start=False, stop=True)
                                off += n_
                            return ins
                        S.add("pe", f_log, reads=KK, writes=pk(bl))
                        S.add("act", lambda e, bl=bl, W_=W_, pt=pt: e.activation(out=pt[:, 0:W_], in_=bank(bl)[:, 0:W_], func=AF.Exp),
                              reads=pk(bl), writes=[ptk])
                        return (g, pt, ptk)

                    def rec_pv(ctx):
                        g, pt, ptk = ctx
                        flags = []
                        for it in g:
                            for (b01, colf, lo, n_) in it[7]:
                                flags.append((first[b01], first[b01 + 2]))
                                first[b01] = False
                                first[b01 + 2] = False

                        def f_pv(e, g=g, pt=pt, flags=flags):
                            off = 0
                            ins = None
                            fi = 0
                            for (keys, q, bm, n_, sp_, v, vk, pieces) in g:
                                for (b01, colf, lo, pn) in pieces:
                                    st_o, st_s = flags[fi]
                                    fi += 1
                                    e.matmul(_piece(colf, bank(b01), pn), lhsT=v,
                                             rhs=pt[:, off + lo:off + lo + pn], start=st_o, stop=True, skip_group_check=True)
                                    ins = e.matmul(_piece(colf, bank(b01 + 2), pn), lhsT=ones_b, rhs=pt[:, off + lo:off + lo + pn],
                                                   start=st_s, stop=True, skip_group_check=True)
                                off += n_
                            return ins
                        vks = []
                        for it in g:
                            vks += it[6]
                        S.add("pe", f_pv, reads=[ptk] + vks + CB, writes=pk(0, 1, 2, 3))

                    pend = []
                    for g in groups:
                        pend.append(rec_log(g))
                        if len(pend) > 2:
                            rec_pv(pend.pop(0))
                    while pend:
                        rec_pv(pend.pop(0))
                    S.add("dve", lambda e: e.reciprocal(out=rsb, in_=ps[1][:, :]), reads=pk(2, 3), writes=["rsb"])
                    S.add("dve", lambda e, h=h: e.tensor_tensor(out=mix(8 + h), in0=ps[0][:, :], in1=rsb, op=ALU.mult),
                          reads=pk(0, 1) + ["rsb"], writes=[("mix", 8 + h)])
                if li == 0:
                    dump("mixb", mix(9), [("mix", 9)])
                if stop == 4:
                    raise _Stop()
                S.barrier()

                sr = ff(8192, 2048).rearrange("p (a b) -> p a b", b=1024)
                osb = ff(10240, 2048).rearrange("p (a b) -> p a b", b=1024)
                qcl = bf(12288, 1024)
                Smf = ff(12800, 256)
                Smb = bf(13056, 256)
                sq4 = bf(13184, 2048).rearrange("p (a b) -> p a b", b=1024)
                rstd4 = ff(14208, 1024)
                t4 = ff(15232, 1024)
                for h in range(4):
                    wv, wkey = wget(P["ra"][h])
                    S.add("sp", lambda e, h=h: e.dma_start(out=osb, in_=park_o[l].ap()[h].rearrange("p (a t) -> p a t", a=2)),
                          reads=[("parko", l, h, n) for n in range(8)], writes=[("osb", 0), ("osb", 1)], kind="dma")
                    S.add("sp", lambda e, h=h: e.dma_start(out=qcl, in_=park_q[l].ap()[h]), reads=[("parkq", l, h)], writes=["qcl"], kind="dma")
                    S.add("sp", lambda e, h=h: e.dma_start(out=Smf, in_=sfa[l].ap()[h * 128:(h + 1) * 128, :]), reads=[("sfa", l)], writes=["Smf"], kind="dma")
                    S.add("dve", lambda e: e.tensor_scalar(out=Smb, in0=Smf, scalar1=flag, scalar2=None, op0=ALU.mult), reads=["Smf", "gpar"], writes=["Smb"])
                    for vc in range(2):
                        pair = prot4.next()
                        proj_fm(wv, wkey, vc * 128, pair)
                        S.add("act", lambda e, vc=vc, pair=pair: e.activation(out=sr[:, vc, :], in_=ps[pair][:, :], func=AF.Silu),
                              reads=pk(2 * pair, 2 * pair + 1), writes=[("sr", vc)])
                    for vc in range(2):
                        pair = prot4.next()
                        for t in range(2):
                            S.add("pe", lambda e, vc=vc, pair=pair, t=t: e.matmul(ps[pair][:, t * 512:(t + 1) * 512], lhsT=Smb[:, vc * 128:(vc + 1) * 128],
                                                                                  rhs=qcl[:, t * 512:(t + 1) * 512], start=True, stop=True),
                                  reads=["Smb", "qcl"], writes=pk(2 * pair + t))
                        S.add("dve", lambda e, vc=vc, pair=pair: e.tensor_tensor(out=osb[:, vc, :], in0=ps[pair][:, :], in1=osb[:, vc, :], op=ALU.add),
                              reads=pk(2 * pair, 2 * pair + 1) + [("osb", vc)], writes=[("osb", vc)])
                        S.add("act", lambda e, vc=vc: e.activation(out=sq4[:, vc, :], in_=osb[:, vc, :], func=AF.Square), reads=[("osb", vc)], writes=[("sq4", vc)])
                    pair = prot4.next()
                    for t in range(2):
                        def fn(e, t=t, pair=pair):
                            e.matmul(ps[pair][:, t * 512:(t + 1) * 512], lhsT=ones_b, rhs=sq4[:, 0, t * 512:(t + 1) * 512], start=True, stop=False)
                            return e.matmul(ps[pair][:, t * 512:(t + 1) * 512], lhsT=ones_b, rhs=sq4[:, 1, t * 512:(t + 1) * 512], start=False, stop=True)
                        S.add("pe", fn, reads=[("sq4", 0), ("sq4", 1)] + CB, writes=pk(2 * pair + t))
                    S.add("act", lambda e, pair=pair: e.activation(out=rstd4, in_=ps[pair][:, :], func=AF.Sqrt, bias=float(256 * EPS)),
                          reads=pk(2 * pair, 2 * pair + 1), writes=["rstd4"])
                    S.add("dve", lambda e: e.reciprocal(out=rstd4, in_=rstd4), reads=["rstd4"], writes=["rstd4"])
                    for vc in range(2):
                        S.add("dve", lambda e, vc=vc: e.scalar_tensor_tensor(out=t4, in0=osb[:, vc, :], scalar=gsc[:, 64 + l * 2 + vc:64 + l * 2 + vc + 1], in1=rstd4,
                                                                             op0=ALU.mult, op1=ALU.mult),
                              reads=[("osb", vc), "rstd4"] + GS, writes=["t4"])
                        S.add("dve", lambda e, vc=vc, h=h: e.tensor_tensor(out=mix(2 * h + vc), in0=t4, in1=sr[:, vc, :], op=ALU.mult),
                              reads=["t4", ("sr", vc)], writes=[("mix", 2 * h + vc)])
                if li == 0:
                    dump("mixa", mix(1), [("mix", 1)])
                MIXK = [("mix", k) for k in range(16)]
                prot3 = Rot([0, 1, 2])
                pend_n = []
                for s8 in range(8):
                    wv, wkey = wget(P["wo"][s8])
                    for nn in range(2):
                        n = 2 * s8 + nn
                        pair = prot3.next()
                        for t in range(2):
                            def fn(e, t=t, pair=pair, nn=nn, wv=wv):
                                ins = None
                                for k in range(16):
                                    ins = e.matmul(ps[pair][:, t * 512:(t + 1) * 512], lhsT=wv[:, k, nn * 128:(nn + 1) * 128],
                                                   rhs=mix(k)[:, t * 512:(t + 1) * 512], start=(k == 0), stop=(k == 15))
                                return ins
                            S.add("pe", fn, reads=[wkey] + MIXK, writes=pk(2 * pair + t))
                        while len(pend_n) > 1:
                            norm_stats_chunk(pend_n.pop(0), 8192, [("sr", 0)])
                        S.add("dve", lambda e, n=n, pair=pair: e.tensor_tensor(out=xT[:, n, :], in0=ps[pair][:, :], in1=xT[:, n, :], op=ALU.add),
                              reads=pk(2 * pair, 2 * pair + 1) + [("x", n, 0), ("x", n, 1)], writes=[("x", n, 0), ("x", n, 1)])
                        pend_n.append(n)
                while pend_n:
                    norm_stats_chunk(pend_n.pop(0), 8192, [("sr", 0)])
                if li == 0:
                    dump("x1", xT[:, 5, :], [("x", 5, 0), ("x", 5, 1)])
                if stop == 5:
                    raise _Stop()
                S.barrier()
                norm_finish(32 + l * 16)
                if stop == 6:
                    raise _Stop()
                S.barrier()
                aT = bf(0, 24 * 1024).rearrange("p (a b) -> p a b", b=1024)
                sg = [ff(12288 + 1024 * i, 1024) for i in range(2)]
                gi = 0
                for part, (j0, nj) in enumerate(FFN_PARTS):
                    gu, dn = P["ffn"][part]
                    for s in range(nj // 2):
                        wg, kg = wget(gu[s][0])
                        wu, ku = wget(gu[s][1])
                        for jj in range(2):
                            j = 2 * s + jj
                            pg, pu = prot4.next(), prot4.next()
                            proj_fm(wg, kg, jj * 128, pg)
                            proj_fm(wu, ku, jj * 128, pu)
                            sgi = sg[gi % 2]
                            S.add("act", lambda e, pg=pg, sgi=sgi: e.activation(out=sgi, in_=ps[pg][:, :], func=AF.Silu),
                                  reads=pk(2 * pg, 2 * pg + 1), writes=[("sg", gi % 2)])
                            S.add("dve", lambda e, pu=pu, sgi=sgi, j=j: e.tensor_tensor(out=aT[:, j, :], in0=ps[pu][:, :], in1=sgi, op=ALU.mult),
                                  reads=pk(2 * pu, 2 * pu + 1) + [("sg", gi % 2)], writes=[("aT", j)])
                            gi += 1
                    AK = [("aT", j) for j in range(nj)]
                    fuse = fuse_next and part == len(FFN_PARTS) - 1
                    protB = Rot([0, 1, 2]) if fuse else prot4
                    pend_n = []
                    for n in range(16):
                        wd, kd = wget(dn[n])
                        pair = protB.next()

                        def fn(e, pair=pair, wd=wd, nj=nj):
                            ins = None
                            for j in range(nj):
                                for t in range(2):
                                    ins = e.matmul(ps[pair][:, t * 512:(t + 1) * 512], lhsT=wd[:, j, :], rhs=aT[:, j, t * 512:(t + 1) * 512],
                                                   start=(j == 0), stop=(j == nj - 1))
                            return ins
                        S.add("pe", fn, reads=[kd] + AK, writes=pk(2 * pair, 2 * pair + 1))
                        while fuse and len(pend_n) > 1:
                            norm_stats_chunk(pend_n.pop(0), 14336, [])
                        S.add("dve", lambda e, n=n, pair=pair: e.tensor_tensor(out=xT[:, n, :], in0=ps[pair][:, :], in1=xT[:, n, :], op=ALU.add),
                              reads=pk(2 * pair, 2 * pair + 1) + [("x", n, 0), ("x", n, 1)], writes=[("x", n, 0), ("x", n, 1)])
                        pend_n.append(n)
                        if (not fuse_next) and part == len(FFN_PARTS) - 1 and stop is None:
                            S.add("sp", lambda e, n=n: e.dma_start(out=yT_d[n * 128:(n + 1) * 128, :], in_=xT[:, n, :]),
                                  reads=[("x", n, 0), ("x", n, 1)], kind="dma")
                            out_done[0] = True
                    while fuse and pend_n:
                        norm_stats_chunk(pend_n.pop(0), 14336, [])
                    S.barrier()
            for li_0, l_0 in enumerate(layers):
                do_layer(li_0, l_0, li_0 > 0, li_0 + 1 < len(layers))
        except _Stop:
            pass
        for q in range(0 if not out_done[0] else 4, 4):
            S.add("sp", lambda e, q=q: e.dma_start(
                out=yT_d[512 * q:512 * (q + 1), :].rearrange("(c p) t -> p c t", p=128), in_=xT[:, 4 * q:4 * q + 4, :]),
                reads=[("x", c, t) for c in range(4 * q, 4 * q + 4) for t in range(2)], kind="dma")
        S.emit()
    return nc


MODE = "fused"
_CACHE = {}


def _get_prog(layers, dbg=()):
    key = (tuple(layers), tuple(dbg))
    if key not in _CACHE:
        _CACHE[key] = build(layers, dbg)
    return _CACHE[key]


def _prep_maps(inputs, xTs, layers=(0, 1), names=None):
    f = lambda k: np.ascontiguousarray(np.asarray(inputs[k], dtype=np.float32))
    fw = lambda k: np.ascontiguousarray(np.asarray(inputs[k], dtype=np.float32)[list(layers)])
    w_in, w_out, w_gate, w_up, w_down = fw("w_in"), fw("w_out"), fw("w_gate"), fw("w_up"), fw("w_down")
    g1 = f("norm1_g").reshape(2, 16, 128).transpose(2, 0, 1).reshape(128, 32)
    g2 = f("norm2_g").reshape(2, 16, 128).transpose(2, 0, 1).reshape(128, 32)
    og = f("gla_onorm_g").reshape(2, 2, 128).transpose(2, 0, 1).reshape(128, 4)
    qg = f("q_norm_g").T
    kg = f("k_norm_g").T
    w2 = np.ascontiguousarray(f("gla_gate_w2").transpose(1, 0, 2).reshape(16, 1024))
    gb = np.ascontiguousarray(f("gla_gate_b").reshape(1, 1024))
    rel_bias = f("rel_bias")
    cst = _consts()
    maps = []
    for c in range(8):
        half = c % 2
        gpar = np.zeros((128, 80), np.float32)
        gpar[:, 0:32], gpar[:, 32:64], gpar[:, 64:68] = g1, g2, og
        gpar[:, 68:70], gpar[:, 70:72] = qg, kg
        gpar[:, 72] = float(half)
        idx, msk = _bias_index_and_mask(half)
        biasT = np.ascontiguousarray(rel_bias[idx, :].transpose(2, 0, 1))
        m = {"xT": xTs[c], "w_in": w_in, "w_out": w_out, "w_gate": w_gate, "w_up": w_up, "w_down": w_down,
             "gpar": gpar, "w2": w2, "gb": gb, "cst": cst, "biasT": biasT, "maskT": msk}
        if names is not None:
            m = {k: v for k, v in m.items() if k in names}
        maps.append(m)
    return maps


def kernel(**inputs):
    x = np.asarray(inputs["x"], dtype=np.float32)
    xTs = [np.ascontiguousarray(x[c // 2, (c % 2) * T:(c % 2 + 1) * T, :].T) for c in range(8)]
    groups = [(0, 1)] if MODE == "fused" else [(0,), (1,)]
    for layers in groups:
        nc = _get_prog(layers)
        maps = _prep_maps(inputs, xTs, layers)
        res = run_bass_kernel_spmd(nc, maps, core_ids=list(range(8)))
        xTs = [np.ascontiguousarray(res.results[c]["yT"]) for c in range(8)]
    out = np.empty_like(x)
    for c in range(8):
        out[c // 2, (c % 2) * T:(c % 2 + 1) * T, :] = xTs[c].T
    return out
```
